# Optimizing a Trainium2 kernel written in Bass

```python
import math
import jax, jax.numpy as jnp
from jax import lax
import numpy as np


D_MODEL = 2048
BATCH = 8
SEQ = 2048
DEPTH = 1

CHUNK = 64
Q_BLOCK = 128
N_MEM = 256
HG_HEADS = 16
HG_KDIM = 128
HG_VDIM = D_MODEL // HG_HEADS
HG_K = HG_HEADS * HG_KDIM
HG_V = HG_HEADS * HG_VDIM
MLA_HEADS = 16
Q_LORA = 512
KV_LORA = 512
QK_NOPE = 128
QK_ROPE = 64
V_HEAD = 128
MLA_QK = QK_NOPE + QK_ROPE
MLA_V = MLA_HEADS * V_HEAD
ROPE_THETA = 10000.0
XA_HEADS = 4
XA_HEAD_DIM = 128
XA_WIDTH = XA_HEADS * XA_HEAD_DIM
D_FF = 5632
FFN_RESIDUAL_WEIGHT = 0.5
EPS = 1e-6
IN_SPLITS = (HG_K, HG_K, HG_V, HG_V, Q_LORA, KV_LORA, QK_ROPE, D_MODEL, D_MODEL)
IN_DIM = HG_K + HG_K + HG_V + HG_V + Q_LORA + KV_LORA + QK_ROPE + D_MODEL + D_MODEL

kernel_name = 'hybrid_hgrn2_mla_macaron_sandwich_memory_block'


def rms_norm(x, g):
    xf = x.astype(jnp.float32)
    y = xf * lax.rsqrt(jnp.mean(xf * xf, axis=-1, keepdims=True) + EPS)
    return (y * g.astype(jnp.float32)).astype(x.dtype)


def rotary(x, cos, sin):
    x1, x2 = jnp.split(x, 2, axis=-1)
    return jnp.concatenate([x1 * cos - x2 * sin, x2 * cos + x1 * sin], axis=-1)


def split_columns(u):
    parts, start = [], 0
    for width in IN_SPLITS:
        parts.append(u[..., start:start + width])
        start += width
    return parts


def swiglu_half_step(x, pre_g, w_gate, w_up, w_down, post_g):
    h = rms_norm(x, pre_g)
    y = (jax.nn.silu(h @ w_gate) * (h @ w_up)) @ w_down
    return x + FFN_RESIDUAL_WEIGHT * rms_norm(y, post_g)


def hgrn2_chunk_scan(q, k, log_f, v):
    B, S, H, K = q.shape
    V = v.shape[-1]
    n = S // CHUNK

    def to_chunks(t):
        return t.reshape(B, n, CHUNK, H, t.shape[-1]).transpose(1, 0, 3, 2, 4)

    causal = jnp.tril(jnp.ones((CHUNK, CHUNK), dtype=bool))

    def step(state, inp):
        qc, kc, gc, vc = inp
        b = jnp.cumsum(gc, axis=2)
        diff = b[:, :, :, None, :] - b[:, :, None, :, :]
        decay = jnp.exp(jnp.where(causal[None, None, :, :, None], diff, -jnp.inf))
        scores = jnp.einsum('bhtk,bhsk,bhtsk->bhts', qc, kc, decay)
        o = (jnp.einsum('bhts,bhsv->bhtv', scores, vc)
             + jnp.einsum('bhtk,bhkv->bhtv', qc * jnp.exp(b), state))
        b_last = b[:, :, -1:, :]
        new_state = (jnp.exp(b_last[:, :, 0, :])[..., None] * state
                     + jnp.einsum('bhsk,bhsv->bhkv', kc * jnp.exp(b_last - b), vc))
        return new_state, o

    state0 = jnp.zeros((B, H, K, V), jnp.float32)
    _, o = lax.scan(step, state0, (to_chunks(q), to_chunks(k), to_chunks(log_f), to_chunks(v)))
    return o.transpose(1, 0, 3, 2, 4).reshape(B, S, H, V)


def chunk_causal_attention(q, k, v, scale):
    B, S, H, Dqk = q.shape
    Dv = v.shape[-1]
    nblk = S // Q_BLOCK
    qb = q.reshape(B, nblk, Q_BLOCK, H, Dqk).transpose(1, 0, 2, 3, 4)
    key_chunk = jnp.arange(S) // CHUNK

    def one_block(args):
        blk, q_blk = args
        query_chunk = (blk * Q_BLOCK + jnp.arange(Q_BLOCK)) // CHUNK
        mask = key_chunk[None, :] <= query_chunk[:, None]
        s = jnp.einsum('bqhd,bkhd->bhqk', q_blk, k, preferred_element_type=jnp.float32) * scale
        s = jnp.where(mask[None, None], s, -jnp.inf)
        p = jax.nn.softmax(s, axis=-1).astype(v.dtype)
        return jnp.einsum('bhqk,bkhd->bqhd', p, v)

    out = lax.map(one_block, (jnp.arange(nblk), qb))
    return out.transpose(1, 0, 2, 3, 4).reshape(B, S, H, Dv)


def hybrid_mixer(x, cos, sin, lb, pre_g, w_in, hg_norm_g, q_norm_g, w_q_up, kv_norm_g,
                 w_kv_up, w_branch_a, w_branch_b, w_out, post_g):
    B, S, _ = x.shape
    f32 = jnp.float32
    h = rms_norm(x, pre_g)
    u = h @ w_in
    q_hg, f_hg, i_hg, og_hg, c_q, c_kv, k_pe, gate_a, gate_b = split_columns(u)

    f_raw = f_hg.astype(f32).reshape(B, S, HG_HEADS, HG_KDIM)
    lb_h = lb.reshape(HG_HEADS, HG_KDIM)
    log_f = jnp.logaddexp(jnp.log(lb_h), jnp.log1p(-lb_h) + jax.nn.log_sigmoid(f_raw))
    k_in = (1.0 - lb_h) * jax.nn.sigmoid(-f_raw)
    q_in = jax.nn.silu(q_hg.astype(f32)).reshape(B, S, HG_HEADS, HG_KDIM)
    v_in = i_hg.astype(f32).reshape(B, S, HG_HEADS, HG_VDIM)
    o_a = hgrn2_chunk_scan(q_in, k_in, log_f, v_in).astype(x.dtype)
    o_a = (rms_norm(o_a, hg_norm_g.reshape(HG_HEADS, HG_VDIM))
           * jax.nn.silu(og_hg).reshape(B, S, HG_HEADS, HG_VDIM))
    y_a = o_a.reshape(B, S, HG_V) @ w_branch_a

    q = (rms_norm(c_q, q_norm_g) @ w_q_up).reshape(B, S, MLA_HEADS, MLA_QK)
    q_nope, q_pe = q[..., :QK_NOPE], q[..., QK_NOPE:]
    q_pe = rotary(q_pe, cos[:, :, None, :], sin[:, :, None, :])
    kv = (rms_norm(c_kv, kv_norm_g) @ w_kv_up).reshape(B, S, MLA_HEADS, QK_NOPE + V_HEAD)
    k_nope, v = kv[..., :QK_NOPE], kv[..., QK_NOPE:]
    k_pe = rotary(k_pe, cos, sin)
    q_full = jnp.concatenate([q_nope, q_pe], axis=-1)
    k_full = jnp.concatenate(
        [k_nope, jnp.broadcast_to(k_pe[:, :, None, :], (B, S, MLA_HEADS, QK_ROPE))], axis=-1)
    o_b = chunk_causal_attention(q_full, k_full, v, MLA_QK ** -0.5)
    y_b = o_b.reshape(B, S, MLA_V) @ w_branch_b

    y = jax.nn.sigmoid(gate_a) * y_a + jax.nn.sigmoid(gate_b) * y_b
    return x + rms_norm(y @ w_out, post_g)


def memory_cross_attention(x, mem, pre_g, mem_g, w_q, w_k, w_v, w_o, post_g):
    B, S, _ = x.shape
    M = mem.shape[1]
    h = rms_norm(x, pre_g)
    m = rms_norm(mem, mem_g)
    q = (h @ w_q).reshape(B, S, XA_HEADS, XA_HEAD_DIM)
    k = (m @ w_k).reshape(B, M, XA_HEADS, XA_HEAD_DIM)
    v = (m @ w_v).reshape(B, M, XA_HEADS, XA_HEAD_DIM)
    s = jnp.einsum('bqhd,bkhd->bhqk', q, k, preferred_element_type=jnp.float32) * XA_HEAD_DIM ** -0.5
    p = jax.nn.softmax(s, axis=-1).astype(v.dtype)
    o = jnp.einsum('bhqk,bkhd->bqhd', p, v).reshape(B, S, XA_WIDTH)
    return x + rms_norm(o @ w_o, post_g)


def setup_inputs(seed: int = 0) -> dict:
    key = jax.random.key(seed)
    ks = iter(jax.random.split(key, 40))
    L = DEPTH

    def w(shape, fan_in):
        return jax.random.normal(next(ks), shape, jnp.float32) * fan_in ** -0.5

    def gain(n):
        return 1.0 + 0.02 * jax.random.normal(next(ks), (L, n), jnp.float32)

    x = jax.random.normal(next(ks), (BATCH, SEQ, D_MODEL), jnp.float32)
    mem = jax.random.normal(next(ks), (BATCH, N_MEM, D_MODEL), jnp.float32)
    offset = jax.random.randint(next(ks), (BATCH, 1), 0, 64, dtype=jnp.int32) * CHUNK
    positions = (offset + jnp.arange(SEQ, dtype=jnp.int32)[None, :]).astype(jnp.int32)
    hgrn_lb_logits = 0.5 * jax.random.normal(next(ks), (L + 1, HG_K), jnp.float32)
    return {
        'x': x, 'mem': mem, 'positions': positions, 'hgrn_lb_logits': hgrn_lb_logits,
        'ffn1_pre_g': gain(D_MODEL),
        'ffn1_w_gate': w((L, D_MODEL, D_FF), D_MODEL),
        'ffn1_w_up': w((L, D_MODEL, D_FF), D_MODEL),
        'ffn1_w_down': w((L, D_FF, D_MODEL), D_FF),
        'ffn1_post_g': gain(D_MODEL),
        'mix_pre_g': gain(D_MODEL),
        'w_in': w((L, D_MODEL, IN_DIM), D_MODEL),
        'hg_norm_g': gain(HG_V),
        'mla_q_norm_g': gain(Q_LORA),
        'mla_w_q_up': w((L, Q_LORA, MLA_HEADS * MLA_QK), Q_LORA),
        'mla_kv_norm_g': gain(KV_LORA),
        'mla_w_kv_up': w((L, KV_LORA, MLA_HEADS * (QK_NOPE + V_HEAD)), KV_LORA),
        'w_branch_a': w((L, HG_V, D_MODEL), HG_V),
        'w_branch_b': w((L, MLA_V, D_MODEL), MLA_V),
        'w_out': w((L, D_MODEL, D_MODEL), D_MODEL),
        'mix_post_g': gain(D_MODEL),
        'xa_pre_g': gain(D_MODEL),
        'xa_mem_g': gain(D_MODEL),
        'xa_w_q': w((L, D_MODEL, XA_WIDTH), D_MODEL),
        'xa_w_k': w((L, D_MODEL, XA_WIDTH), D_MODEL),
        'xa_w_v': w((L, D_MODEL, XA_WIDTH), D_MODEL),
        'xa_w_o': w((L, XA_WIDTH, D_MODEL), XA_WIDTH),
        'xa_post_g': gain(D_MODEL),
        'ffn2_pre_g': gain(D_MODEL),
        'ffn2_w_gate': w((L, D_MODEL, D_FF), D_MODEL),
        'ffn2_w_up': w((L, D_MODEL, D_FF), D_MODEL),
        'ffn2_w_down': w((L, D_FF, D_MODEL), D_FF),
        'ffn2_post_g': gain(D_MODEL),
    }


def reference(x, mem, positions, hgrn_lb_logits,
              ffn1_pre_g, ffn1_w_gate, ffn1_w_up, ffn1_w_down, ffn1_post_g,
              mix_pre_g, w_in, hg_norm_g, mla_q_norm_g, mla_w_q_up, mla_kv_norm_g, mla_w_kv_up,
              w_branch_a, w_branch_b, w_out, mix_post_g,
              xa_pre_g, xa_mem_g, xa_w_q, xa_w_k, xa_w_v, xa_w_o, xa_post_g,
              ffn2_pre_g, ffn2_w_gate, ffn2_w_up, ffn2_w_down, ffn2_post_g):
    f32 = jnp.float32
    inv_freq = 1.0 / (ROPE_THETA ** (jnp.arange(0, QK_ROPE, 2, dtype=f32) / QK_ROPE))
    ang = positions.astype(f32)[..., None] * inv_freq
    cos = jnp.cos(ang).astype(x.dtype)
    sin = jnp.sin(ang).astype(x.dtype)
    lower_bounds = jnp.cumsum(jax.nn.softmax(hgrn_lb_logits.astype(f32), axis=0), axis=0)

    for l in range(DEPTH):
        x = swiglu_half_step(x, ffn1_pre_g[l], ffn1_w_gate[l], ffn1_w_up[l], ffn1_w_down[l], ffn1_post_g[l])
        x = hybrid_mixer(x, cos, sin, lower_bounds[l], mix_pre_g[l], w_in[l], hg_norm_g[l],
                         mla_q_norm_g[l], mla_w_q_up[l], mla_kv_norm_g[l], mla_w_kv_up[l],
                         w_branch_a[l], w_branch_b[l], w_out[l], mix_post_g[l])
        x = memory_cross_attention(x, mem, xa_pre_g[l], xa_mem_g[l], xa_w_q[l], xa_w_k[l],
                                   xa_w_v[l], xa_w_o[l], xa_post_g[l])
        x = swiglu_half_step(x, ffn2_pre_g[l], ffn2_w_gate[l], ffn2_w_up[l], ffn2_w_down[l], ffn2_post_g[l])
    return x
```

```python
import contextlib
import numpy as np
import concourse.bass as bass
import concourse.mybir as mybir
from concourse.bass_utils import run_bass_kernel_spmd

F32 = mybir.dt.float32
BF = mybir.dt.bfloat16
I32 = mybir.dt.int32
AF = mybir.ActivationFunctionType
ALU = mybir.AluOpType

D = 2048
S = 2048
DFF = 5632
TT = 512
NT = S // TT
NSUB = TT // 128
KC = D // 128
EPS = 1e-6
IN_DIM = 13376
N_MEM = 256

SAME_ENG_SYNC = True


class Eng:
    def __init__(self, name, h, sem):
        self.name, self.h, self.sem = name, h, sem
        self.count = 0
        self.waited = {}


class Buf:
    __slots__ = ("name", "w", "r", "sem", "dcount")

    def __init__(self, name):
        self.name = name
        self.w = []
        self.r = []
        self.sem = None
        self.dcount = 0


class Cx:
    def __init__(self, nc, st):
        self.nc, self.st = nc, st
        self.E = {}
        for name, h in (("pe", nc.tensor), ("act", nc.scalar), ("dve", nc.vector),
                        ("pool", nc.gpsimd), ("sp", nc.sync)):
            self.E[name] = Eng(name, h, nc.alloc_semaphore(name="sem_" + name))
        self.dma_sems = []
        self.sem_pool = []
        self.nsem = 0
        self.nbuf = 0

    def buf(self, name=None):
        self.nbuf += 1
        return Buf(f"{name or 'b'}_{self.nbuf}")

    def end_phase(self, mark):
        self.barrier()
        for b in self.dma_sems[mark:]:
            self.sem_pool.append((b.sem, b.dcount))
        del self.dma_sems[mark:]

    def sb(self, name, shape, dt):
        self.nbuf += 1
        return self.st.enter_context(self.nc.sbuf_tensor(f"{name}_{self.nbuf}", list(shape), dt))

    def _wait(self, E, dep):
        if dep[0] == "eng":
            X, need = dep[1], dep[2]
            if X is E:
                if E.name == "pe" or not SAME_ENG_SYNC or not dep[3]:
                    return
            assert need <= X.count, f"wait on not-yet-issued inc {X.name} {need}>{X.count}"
            key = X.name
            if E.waited.get(key, 0) >= need:
                return
            E.h.wait_ge(X.sem, need)
            E.waited[key] = need
        else:
            sem, val, key = dep[1], dep[2], dep[3]
            if E.waited.get(key, 0) >= val:
                return
            E.h.wait_ge(sem, val)
            E.waited[key] = val

    def _deps(self, E, reads, writes):
        for b in reads:
            for d in b.w:
                self._wait(E, d)
        for b in writes:
            for d in b.w:
                self._wait(E, d)
            for d in b.r:
                self._wait(E, d)

    def op(self, eng, fn, reads=(), writes=(), inc=True, accum=False):
        E = self.E[eng]
        self._deps(E, reads, writes)
        inst = fn(E.h)
        need = E.count + 1
        if inc:
            inst.then_inc(E.sem, 1)
            E.count += 1
        dep = ("eng", E, need, True)
        for b in reads:
            b.r.append(dep)
        for b in writes:
            if accum:
                b.w.append(dep)
            else:
                b.w = [dep]
                b.r = []
        return inst

    def dma(self, out_ap, in_ap, sbuf, reads=(), writes=(), q="sp", partial=False, **kw):
        E = self.E[q]
        self._deps(E, reads, writes)
        if sbuf.sem is None:
            if self.sem_pool:
                sbuf.sem, sbuf.dcount = self.sem_pool.pop()
            else:
                self.nsem += 1
                sbuf.sem = self.nc.alloc_semaphore(name=f"dsem{self.nsem}")
            self.dma_sems.append(sbuf)
        sbuf.dcount += 16
        E.h.dma_start(out=out_ap, in_=in_ap, **kw).then_inc(sbuf.sem, 16)
        dep = ("dma", sbuf.sem, sbuf.dcount, sbuf.name)
        for b in reads:
            b.r.append(dep)
        for b in writes:
            if partial:
                b.w.append(dep)
            else:
                b.w = [dep]
                b.r = []

    def barrier(self):
        for E in self.E.values():
            for X in self.E.values():
                if X is E or X.count == 0:
                    continue
                if E.waited.get(X.name, 0) < X.count:
                    E.h.wait_ge(X.sem, X.count)
                    E.waited[X.name] = X.count
            for b in self.dma_sems:
                if E.waited.get(b.name, 0) < b.dcount:
                    E.h.wait_ge(b.sem, b.dcount)
                    E.waited[b.name] = b.dcount


HG_H = 16
Q_LORA = 512
KV_LORA = 512
MLA_H = 16
XA_H = 4
TWO_PI = 6.283185307179586
C1 = 6.28125
C2 = TWO_PI - C1


class PH:
    def __init__(self, P, st, pre_g=None, post_g=None, xy=True, nst=2, nwb=3, wcols=256):
        self.P, self.cx = P, P.cx
        cx = self.cx
        cx.st = st
        self.wcols = wcols
        if pre_g is not None:
            self.gpre = cx.sb("gpre", [128, D], F32); self.gpre_b = cx.buf("gpre")
            cx.dma(self.gpre[:], pre_g.partition_broadcast(128), self.gpre_b, writes=[self.gpre_b])
            self.h0 = [cx.sb(f"h0_{i}", [128, D], BF) for i in range(2)]; self.h0_b = [cx.buf(f"h0_{i}") for i in range(2)]
            self.hT = cx.sb("hT", [128, KC, TT], BF); self.hT_b = [cx.buf(f"hT{s}") for s in range(NSUB)]
            self.ssp = cx.sb("ssp", [128, 4], F32); self.ssp_b = cx.buf("ssp")
        if post_g is not None:
            self.gpost = cx.sb("gpost", [128, D], F32); self.gpost_b = cx.buf("gpost")
            cx.dma(self.gpost[:], post_g.partition_broadcast(128), self.gpost_b, writes=[self.gpost_b])
            self.ss = cx.sb("ss", [128, 16], F32); self.ss_b = cx.buf("ss")
        if xy:
            self.xy = cx.sb("xy", [128, NSUB, D], F32); self.xy_b = [cx.buf(f"xy{s}") for s in range(NSUB)]
        self.xr = [cx.sb(f"xr{i}", [128, D], F32) for i in range(2)]; self.xr_b = [cx.buf(f"xr{i}") for i in range(2)]
        self.NST, self.NWB = nst, nwb
        self.wst = [cx.sb(f"wst{i}", [128, KC * wcols], F32) for i in range(nst)]; self.wst_b = [cx.buf(f"wst{i}") for i in range(nst)]
        self.wbf = [cx.sb(f"wbf{i}", [128, KC * wcols], BF) for i in range(nwb)]; self.wbf_b = [cx.buf(f"wbf{i}") for i in range(nwb)]
        self.junks = [cx.sb(f"junk{i}", [128, D], BF) for i in range(2)]; self.junk_bs = [cx.buf(f"junk{i}") for i in range(2)]
        self.rs = cx.sb("rs", [128, 8], F32); self.rs_b = cx.buf("rs"); self.rsq_b = cx.buf("rsq")
        self.jctr = 0; self.cast_rr = 0; self.wctr = 0; self.pbank = 0
        self.cast_engs = ["act", "dve", "pool"]

    def nextjunk(self):
        self.jctr += 1
        return self.junks[self.jctr % 2], self.junk_bs[self.jctr % 2]

    def bank(self):
        b = self.pbank % 8
        self.pbank += 1
        return b

    def cast(self, out_ap, in_ap, reads, writes):
        cx = self.cx
        eng = self.cast_engs[self.cast_rr % len(self.cast_engs)]
        self.cast_rr += 1
        if eng == "act":
            cx.op("act", lambda e: e.copy(out=out_ap, in_=in_ap), reads=reads, writes=writes)
        else:
            cx.op(eng, lambda e: e.tensor_copy(out=out_ap, in_=in_ap), reads=reads, writes=writes)

    def load_w(self, src_ap, k, n):
        cx = self.cx
        i = self.wctr; self.wctr += 1
        s_t, s_b = self.wst[i % self.NST], self.wst_b[i % self.NST]
        w_t, w_b = self.wbf[i % self.NWB], self.wbf_b[i % self.NWB]
        sview = s_t[:, 0:k * n].rearrange("p (k n) -> p k n", k=k)
        wview = w_t[:, 0:k * n].rearrange("p (k n) -> p k n", k=k)
        cx.dma(sview, src_ap, s_b, writes=[s_b])
        self.cast(wview, sview, [s_b], [w_b])
        return wview, w_b

    def load_wcols(self, W, c0, n, kc=KC):
        return self.load_w(W[:, c0:c0 + n].rearrange("(k p) n -> p k n", p=128), kc, n)

    def prenorm(self, xin, xin_bufs, T):
        P, cx = self.P, self.cx
        t0 = T * TT
        for s in range(NSUB):
            r0 = t0 + s * 128
            xt, xb = self.xr[s % 2], self.xr_b[s % 2]
            cx.dma(xt[:], xin[r0:r0 + 128, :], xb, reads=[xin_bufs[T * NSUB + s]], writes=[xb])
            hb, hbb = self.h0[s % 2], self.h0_b[s % 2]
            junk, junk_b = self.nextjunk()
            cx.op("act", lambda e: e.activation(out=junk[:], in_=xt[:], func=AF.Square,
                                                accum_out=self.ssp[:, s:s + 1]),
                  reads=[xb], writes=[junk_b, self.ssp_b])
            P.rstd_from_ss(self.ssp[:, s:s + 1], self.rs[:, s:s + 1], D, [self.ssp_b], [self.rs_b])
            cx.op("dve", lambda e: e.scalar_tensor_tensor(out=hb[:], in0=xt[:], scalar=self.rs[:, s:s + 1],
                                                          in1=self.gpre[:], op0=ALU.mult, op1=ALU.mult),
                  reads=[xb, self.rs_b, self.gpre_b], writes=[hbb])
            P.transpose_to(hb, hbb, KC, lambda c0, n: self.hT[:, c0:c0 + n, s * 128:(s + 1) * 128], self.hT_b[s])

    def proj_fm(self, W, c0, ncols, rhs_fn, rhs_bufs, cb, kc=KC, m=128, nfree=TT):
        P, cx = self.P, self.cx
        done = 0
        while done < ncols:
            n = min(self.wcols, ncols - done)
            wv, wb = self.load_wcols(W, c0 + done, n, kc)
            for jj in range(0, n, m):
                mm = min(m, n - jj)
                pb = self.bank()
                for k in range(kc):
                    cx.op("pe", lambda e: e.matmul(P.ps[pb][0:mm, 0:nfree], lhsT=wv[:, k, jj:jj + mm], rhs=rhs_fn(k),
                                                   start=(k == 0), stop=(k == kc - 1)),
                          reads=[wb] + list(rhs_bufs), writes=[P.psb[pb]], inc=(k == kc - 1), accum=(k > 0))
                cb((done + jj) // m, pb)
            done += n

    def tm_proj_post(self, actT_fn, act_buf_fn, nK, W, T, xres, xres_bufs, xout, xout_bufs, resid_w):
        P, cx = self.P, self.cx
        t0 = T * TT
        xy, xy_b, ss, ss_b = self.xy, self.xy_b, self.ss, self.ss_b
        for half in range(2):
            for j in range(nK):
                wv, wb = self.load_w(W[j * 128:(j + 1) * 128, half * 1024:(half + 1) * 1024]
                                     .rearrange("p (k n) -> p k n", k=1), 1, 1024)
                for s in range(NSUB):
                    for n in range(2):
                        pbk = s * 2 + n
                        last = (j == nK - 1)
                        cx.op("pe", lambda e: e.matmul(P.ps[pbk][:], lhsT=actT_fn(j, s), rhs=wv[:, 0, n * 512:(n + 1) * 512],
                                                       start=(j == 0), stop=last),
                              reads=[wb, act_buf_fn(j)], writes=[P.psb[pbk]],
                              inc=(last or (s == NSUB - 1 and n == 1)), accum=(j > 0))
            for s in range(NSUB):
                for n in range(2):
                    pbk = s * 2 + n
                    col = half * 2 + n
                    junk, junk_b = self.nextjunk()
                    cx.op("act", lambda e: e.activation(out=junk[:, 0:512], in_=P.ps[pbk][:], func=AF.Square,
                                                        accum_out=ss[:, s * 4 + col:s * 4 + col + 1]),
                          reads=[P.psb[pbk]], writes=[junk_b, ss_b, P.psrd[pbk]], accum=True)
                    d0 = half * 1024 + n * 512
                    cx.op("dve", lambda e: e.tensor_tensor(out=xy[:, s, d0:d0 + 512], in0=P.ps[pbk][:],
                                                           in1=self.gpost[:, d0:d0 + 512], op=ALU.mult),
                          reads=[P.psb[pbk], self.gpost_b, P.psrd[pbk]], writes=[xy_b[s]], accum=True)
        for s in range(NSUB):
            r0 = t0 + s * 128
            xrt, xrb = self.xr[s % 2], self.xr_b[s % 2]
            cx.dma(xrt[:], xres[r0:r0 + 128, :], xrb, reads=[xres_bufs[T * NSUB + s]], writes=[xrb])
            cx.op("dve", lambda e: e.reduce_sum(out=self.rs[:, 4 + s:5 + s], in_=ss[:, s * 4:s * 4 + 4],
                                                axis=mybir.AxisListType.X),
                  reads=[ss_b], writes=[self.rsq_b])
            P.rstd_from_ss(self.rs[:, 4 + s:5 + s], self.rs[:, 4 + s:5 + s], D, [self.rsq_b], [self.rsq_b],
                           scale_extra=resid_w)
            cx.op("dve", lambda e: e.scalar_tensor_tensor(out=xrt[:], in0=xy[:, s, :], scalar=self.rs[:, 4 + s:5 + s],
                                                          in1=xrt[:], op0=ALU.mult, op1=ALU.add),
                  reads=[xy_b[s], self.rsq_b, xrb], writes=[xrb])
            cx.dma(xout[r0:r0 + 128, :], xrt[:], xrb, reads=[xrb], writes=[xout_bufs[T * NSUB + s]])
        for s in range(NSUB):
            xy_b[s].w = list(xy_b[s].w)


class Prog:
    def __init__(self, stages=("ffn1", "mix", "mla", "merge", "xa", "ffn2")):
        self.stages = stages
        self.nc = bass.Bass("TRN2", target_bir_lowering=False)
        self.st = contextlib.ExitStack()

    def dram_in(self, name, shape, dt=F32):
        return self.nc.dram_tensor(name, list(shape), dt, kind="ExternalInput").ap()

    def dram_tmp(self, name, shape, dt):
        return self.nc.dram_tensor(name, list(shape), dt, kind="Internal").ap()

    def build(self):
        nc = self.nc
        with self.st as st:
            cx = self.cx = Cx(nc, st)
            I = self.I = {}
            I["x"] = self.dram_in("x", [S, D])
            I["mem"] = self.dram_in("mem", [N_MEM, D])
            I["positions"] = self.dram_in("positions", [1, S], I32)
            I["hgrn_lb_logits"] = self.dram_in("hgrn_lb_logits", [2, D])
            for n in ("ffn1", "ffn2"):
                I[n + "_pre_g"] = self.dram_in(n + "_pre_g", [1, D])
                I[n + "_w_gate"] = self.dram_in(n + "_w_gate", [D, DFF])
                I[n + "_w_up"] = self.dram_in(n + "_w_up", [D, DFF])
                I[n + "_w_down"] = self.dram_in(n + "_w_down", [DFF, D])
                I[n + "_post_g"] = self.dram_in(n + "_post_g", [1, D])
            for n, shp in (("mix_pre_g", [1, D]), ("w_in", [D, IN_DIM]), ("hg_norm_g", [1, D]),
                           ("mla_q_norm_g", [1, Q_LORA]), ("mla_w_q_up", [Q_LORA, MLA_H * 192]),
                           ("mla_kv_norm_g", [1, KV_LORA]), ("mla_w_kv_up", [KV_LORA, MLA_H * 256]),
                           ("w_branch_a", [D, D]), ("w_branch_b", [D, D]), ("w_out", [D, D]), ("mix_post_g", [1, D]),
                           ("xa_pre_g", [1, D]), ("xa_mem_g", [1, D]), ("xa_w_q", [D, 512]), ("xa_w_k", [D, 512]),
                           ("xa_w_v", [D, 512]), ("xa_w_o", [512, D]), ("xa_post_g", [1, D])):
                I[n] = self.dram_in(n, shp)
            I["ident"] = self.dram_in("ident_in", [128, 128])
            I["tri"] = self.dram_in("tri_in", [128, 128])
            I["cmask"] = self.dram_in("cmask_in", [128, 128])
            I["invf"] = self.dram_in("invf_in", [64, 1])
            self.out = nc.dram_tensor("out", [S, D], F32, kind="ExternalOutput").ap()
            nb = S // 128
            self.out_bufs = [cx.buf(f"out{i}") for i in range(nb)]
            self.x_bufs = [cx.buf(f"xin{i}") for i in range(nb)]
            X = {}
            for n in ("X1", "X2", "X3"):
                X[n] = (self.dram_tmp(n, [S, D], F32), [cx.buf(f"{n}_{i}") for i in range(nb)])
            self.YA = self.dram_tmp("YA", [D, S], BF); self.YA_b = [cx.buf(f"YA{i}") for i in range(NT)]
            self.GB = self.dram_tmp("GB", [D, S], BF); self.GB_b = [cx.buf(f"GB{i}") for i in range(NT)]
            self.OB = self.dram_tmp("OB", [D, S], BF); self.OB_b = [cx.buf(f"OB{i}") for i in range(NT)]
            self.CQ = self.dram_tmp("CQ", [Q_LORA, S], BF); self.CQ_b = [cx.buf(f"CQ{i}") for i in range(NT)]
            self.CKV = self.dram_tmp("CKV", [KV_LORA, S], BF); self.CKV_b = [cx.buf(f"CKV{i}") for i in range(NT)]
            self.KR = self.dram_tmp("KR", [64, S], BF); self.KR_b = [cx.buf(f"KR{i}") for i in range(NT)]
            self.CS = self.dram_tmp("CS", [2, 64, S], F32); self.CS_b = cx.buf("CS")

            self.ps = [st.enter_context(nc.psum_tensor(f"ps{i}", [128, 512], F32)) for i in range(8)]
            self.psb = [cx.buf(f"psb{i}") for i in range(8)]
            self.psrd = [cx.buf(f"psrd{i}") for i in range(8)]
            self.ident = cx.sb("ident", [128, 128], BF); self.ident_b = cx.buf("ident")
            self.tri = cx.sb("tri", [128, 128], F32); self.tri_b = cx.buf("tri")
            self.cmask = cx.sb("cmask", [128, 128], BF); self.cmask_b = cx.buf("cmask")
            self.ones = cx.sb("ones", [128, 128], BF); self.ones_b = cx.buf("ones")
            tmpf = cx.sb("tmpf", [128, 128], F32); tmpf_b = cx.buf("tmpf")
            cx.dma(tmpf[:], I["ident"], tmpf_b, writes=[tmpf_b])
            cx.op("dve", lambda e: e.tensor_copy(out=self.ident[:], in_=tmpf[:]), reads=[tmpf_b], writes=[self.ident_b])
            cx.dma(tmpf[:], I["cmask"], tmpf_b, writes=[tmpf_b])
            cx.op("dve", lambda e: e.tensor_copy(out=self.cmask[:], in_=tmpf[:]), reads=[tmpf_b], writes=[self.cmask_b])
            cx.dma(self.tri[:], I["tri"], self.tri_b, writes=[self.tri_b])
            cx.op("pool", lambda e: e.memset(self.ones[:], 1.0), writes=[self.ones_b])
            self.eps_ap(1.0); self.eps_ap(0.5)
            cx.barrier()

            chain = [("ffn1", I["x"], self.x_bufs)]
            cur, cur_b = I["x"], self.x_bufs
            order = [s_ for s_ in ("ffn1", "mix", "xa", "ffn2") if (s_ in self.stages) or (s_ == "mix" and "merge" in self.stages)]
            nxt = {"ffn1": "X1", "mix": "X2", "xa": "X3"}
            for i, sname in enumerate(order):
                lastst = (i == len(order) - 1)
                dst, dst_b = (self.out, self.out_bufs) if lastst else X[nxt[sname]]
                if sname in ("ffn1", "ffn2"):
                    self.ffn_phase(sname, cur, cur_b, dst, dst_b)
                elif sname == "mix":
                    self.rope_tables()
                    self.mix_phase(cur, cur_b)
                    self.mla_phase()
                    self.merge_phase(cur, cur_b, dst, dst_b)
                elif sname == "xa":
                    self.xa_phase(cur, cur_b, dst, dst_b)
                cur, cur_b = dst, dst_b

            sp = cx.E["sp"]
            for b in self.out_bufs:
                for d in b.w:
                    cx._wait(sp, d)
        return nc

    def rstd_from_ss(self, ss_ap, out_ap, n, reads, writes, scale_extra=1.0):
        cx = self.cx
        cx.op("act", lambda e: e.activation(out=out_ap, in_=ss_ap, func=AF.Sqrt,
                                             scale=1.0 / (n * scale_extra * scale_extra),
                                             bias=self.eps_ap(scale_extra)),
              reads=reads, writes=writes)
        cx.op("dve", lambda e: e.reciprocal(out=out_ap, in_=out_ap), reads=writes, writes=writes)

    def eps_ap(self, scale_extra):
        key = float(scale_extra)
        if not hasattr(self, "_eps"):
            self._eps = {}
        if key not in self._eps:
            t = self.st.enter_context(self.nc.sbuf_tensor(f"eps{len(self._eps)}", [128, 1], F32))
            b = self.cx.buf("eps")
            self.cx.op("pool", lambda e: e.memset(t[:], EPS / (key * key)), writes=[b])
            self._eps[key] = (t, b)
        return self._eps[key][0][:]

    def transpose_to(self, src, src_b, nchunks, dst_fn, dst_b, np_in=128):
        cx = self.cx
        done = 0
        first = True
        while done < nchunks:
            n = min(8, nchunks - done)
            pb = self._tbank = (getattr(self, "_tbank", -1) + 1) % 2
            pv = self.ps[pb][:].bitcast(BF)
            for c in range(n):
                cc = done + c
                cx.op("pe", lambda e: e.transpose(out=pv[:, c * 128:c * 128 + np_in],
                                                  in_=src[0:np_in, cc * 128:(cc + 1) * 128],
                                                  identity=self.ident[0:np_in, 0:np_in]),
                      reads=[src_b, self.ident_b], writes=[self.psb[pb]], inc=(c == n - 1), accum=(c > 0))
            srcv = pv[:, 0:n * 128].rearrange("p (c t) -> p c t", c=n)[:, :, 0:np_in]
            eng = "dve" if (done // 8) % 2 == 0 else "act"
            dst = dst_fn(done, n)
            if eng == "act":
                cx.op("act", lambda e: e.copy(out=dst, in_=srcv), reads=[self.psb[pb]], writes=[dst_b], accum=not first)
            else:
                cx.op("dve", lambda e: e.tensor_copy(out=dst, in_=srcv), reads=[self.psb[pb]], writes=[dst_b],
                      accum=not first)
            first = False
            done += n

    def ffn_phase(self, name, xin, xin_bufs, xout, xout_bufs):
        cx, I = self.cx, self.I
        wg, wu, wd = I[name + "_w_gate"], I[name + "_w_up"], I[name + "_w_down"]
        with contextlib.ExitStack() as st:
            old_st = cx.st; mark = len(cx.dma_sems)
            ph = PH(self, st, I[name + "_pre_g"], I[name + "_post_g"])
            NJ = DFF // 128
            actT = cx.sb("actT", [128, NJ, TT], BF); actT_b = [cx.buf(f"actT{j}") for j in range(NJ)]
            sg = [cx.sb(f"sg{i}", [128, TT], F32) for i in range(2)]; sg_b = [cx.buf(f"sg{i}") for i in range(2)]
            for T in range(NT):
                ph.prenorm(xin, xin_bufs, T)
                for g in range(DFF // 256):
                    wgv, wgb = ph.load_wcols(wg, g * 256, 256)
                    wuv, wub = ph.load_wcols(wu, g * 256, 256)
                    for jj in range(2):
                        j = g * 2 + jj
                        pg, pu = ph.bank(), ph.bank()
                        for (wv, wb, pbk) in ((wgv, wgb, pg), (wuv, wub, pu)):
                            for k in range(KC):
                                cx.op("pe", lambda e: e.matmul(self.ps[pbk][:], lhsT=wv[:, k, jj * 128:(jj + 1) * 128],
                                                               rhs=ph.hT[:, k, :], start=(k == 0), stop=(k == KC - 1)),
                                      reads=[wb] + ph.hT_b, writes=[self.psb[pbk]], inc=(k == KC - 1), accum=(k > 0))
                        sgt, sgb = sg[j % 2], sg_b[j % 2]
                        cx.op("act", lambda e: e.activation(out=sgt[:], in_=self.ps[pg][:], func=AF.Silu),
                              reads=[self.psb[pg]], writes=[sgb])
                        cx.op("dve", lambda e: e.tensor_tensor(out=actT[:, j, :], in0=self.ps[pu][:], in1=sgt[:],
                                                               op=ALU.mult),
                              reads=[self.psb[pu], sgb], writes=[actT_b[j]])
                ph.tm_proj_post(lambda j, s: actT[:, j, s * 128:(s + 1) * 128], lambda j: actT_b[j], NJ, wd, T,
                                xin, xin_bufs, xout, xout_bufs, 0.5)
            cx.end_phase(mark)
            cx.st = old_st

    def vecT(self, name, src, ncol):
        cx = self.cx
        t = cx.sb(name, [128, ncol], F32); b = cx.buf(name)
        v = src.rearrange("o (c p) -> p (o c)", p=128)
        for c in range(ncol):
            cx.dma(t[:, c:c + 1], v[:, c:c + 1], b, writes=[b], partial=(c > 0), allow_slow_non_contiguous=True)
        return t, b

    def rope_tables(self):
        cx, I = self.cx, self.I
        PI = 3.141592653589793
        with contextlib.ExitStack() as st:
            old_st, cx.st = cx.st, st; mark = len(cx.dma_sems)
            posi = cx.sb("posi", [64, S], I32); b0 = cx.buf("posi")
            cx.dma(posi[:], I["positions"].partition_broadcast(64), b0, writes=[b0])
            invf = cx.sb("invf", [64, 1], F32); b1 = cx.buf("invf")
            cx.dma(invf[:], I["invf"], b1, writes=[b1])
            ang = cx.sb("ang", [64, S], F32); ba = cx.buf("ang")
            cx.op("dve", lambda e: e.tensor_copy(out=ang[:], in_=posi[:]), reads=[b0], writes=[ba])
            cx.op("dve", lambda e: e.tensor_scalar(out=ang[:], in0=ang[:], scalar1=invf[:, 0:1], scalar2=None,
                                                   op0=ALU.mult), reads=[ba, b1], writes=[ba])
            a2 = cx.sb("a2", [64, S], F32); b2 = cx.buf("a2")
            ki = cx.sb("ki", [64, S], I32); bk = cx.buf("ki")
            kf = cx.sb("kf", [64, S], F32); bf_ = cx.buf("kf")
            r = cx.sb("r", [64, S], F32); br = cx.buf("r")
            m = cx.sb("m", [64, S], F32); bm = cx.buf("m")
            for which, shift in ((0, PI / 2), (1, 0.0)):
                cx.op("dve", lambda e: e.tensor_scalar(out=a2[:], in0=ang[:], scalar1=shift, scalar2=None, op0=ALU.add),
                      reads=[ba], writes=[b2])
                cx.op("dve", lambda e: e.tensor_scalar(out=kf[:], in0=a2[:], scalar1=1.0 / TWO_PI, scalar2=None,
                                                       op0=ALU.mult), reads=[b2], writes=[bf_])
                cx.op("dve", lambda e: e.tensor_copy(out=ki[:], in_=kf[:]), reads=[bf_], writes=[bk])
                cx.op("dve", lambda e: e.tensor_copy(out=kf[:], in_=ki[:]), reads=[bk], writes=[bf_])
                cx.op("dve", lambda e: e.scalar_tensor_tensor(out=r[:], in0=kf[:], scalar=-C1, in1=a2[:],
                                                              op0=ALU.mult, op1=ALU.add), reads=[bf_, b2], writes=[br])
                cx.op("dve", lambda e: e.scalar_tensor_tensor(out=r[:], in0=kf[:], scalar=-C2, in1=r[:],
                                                              op0=ALU.mult, op1=ALU.add), reads=[bf_, br], writes=[br])
                cx.op("dve", lambda e: e.tensor_scalar(out=m[:], in0=r[:], scalar1=PI, scalar2=TWO_PI, op0=ALU.is_gt,
                                                       op1=ALU.mult), reads=[br], writes=[bm])
                cx.op("dve", lambda e: e.tensor_tensor(out=r[:], in0=r[:], in1=m[:], op=ALU.subtract),
                      reads=[br, bm], writes=[br])
                cx.op("dve", lambda e: e.tensor_scalar(out=m[:], in0=r[:], scalar1=-PI, scalar2=TWO_PI, op0=ALU.is_lt,
                                                       op1=ALU.mult), reads=[br], writes=[bm])
                cx.op("dve", lambda e: e.tensor_tensor(out=r[:], in0=r[:], in1=m[:], op=ALU.add),
                      reads=[br, bm], writes=[br])
                cx.op("act", lambda e: e.activation(out=m[:], in_=r[:], func=AF.Sin), reads=[br], writes=[bm])
                cx.dma(self.CS[which], m[:], bm, reads=[bm], writes=[self.CS_b], partial=True)
            cx.end_phase(mark)
            cx.st = old_st

    def mix_phase(self, xin, xin_bufs):
        cx, I = self.cx, self.I
        W = I["w_in"]
        OFF_Q, OFF_F, OFF_I, OFF_OG, OFF_CQ, OFF_CKV, OFF_KPE, OFF_GA, OFF_GB = 0, 2048, 4096, 6144, 8192, 8704, 9216, 9280, 11328
        with contextlib.ExitStack() as st:
            old_st = cx.st; mark = len(cx.dma_sems)
            ph = PH(self, st, I["mix_pre_g"], None, xy=False, nwb=2)
            lg, lg_b = self.vecT("lg", I["hgrn_lb_logits"][0:1, :], 16)
            lg1, lg1_b = self.vecT("lg1", I["hgrn_lb_logits"][1:2, :], 16)
            hgT, hgT_b = self.vecT("hgT", I["hg_norm_g"], 16)
            qgT, qgT_b = self.vecT("qgT", I["mla_q_norm_g"], 4)
            kgT, kgT_b = self.vecT("kgT", I["mla_kv_norm_g"], 4)
            lb = cx.sb("lb", [128, 16], F32); oml = cx.sb("oml", [128, 16], F32); noml = cx.sb("noml", [128, 16], F32)
            lb_b = cx.buf("lb")
            cx.op("dve", lambda e: e.tensor_tensor(out=lb[:], in0=lg[:], in1=lg1[:], op=ALU.subtract),
                  reads=[lg_b, lg1_b], writes=[lb_b])
            cx.op("act", lambda e: e.activation(out=lb[:], in_=lb[:], func=AF.Sigmoid), reads=[lb_b], writes=[lb_b])
            cx.op("dve", lambda e: e.tensor_scalar(out=oml[:], in0=lb[:], scalar1=-1.0, scalar2=1.0, op0=ALU.mult,
                                                   op1=ALU.add), reads=[lb_b], writes=[lb_b], accum=True)
            cx.op("dve", lambda e: e.tensor_scalar(out=noml[:], in0=oml[:], scalar1=-1.0, scalar2=None, op0=ALU.mult),
                  reads=[lb_b], writes=[lb_b], accum=True)
            rmask = cx.sb("rmask", [128, TT], F32); rmask_b = cx.buf("rmask")
            cx.op("pool", lambda e: e.memset(rmask[:], 1.0), writes=[rmask_b])
            cx.op("pool", lambda e: e.memset(rmask[:].rearrange("p (a b) -> p a b", b=64)[:, :, 0:1], 0.0),
                  writes=[rmask_b], accum=True)
            GH = 4
            f32t = {}
            for n in ("sgm", "lf", "bb", "bp", "eb", "enb"):
                f32t[n] = (cx.sb("t_" + n, [128, TT], F32), cx.buf("t_" + n))
            qtT = cx.sb("qtT", [128, GH, TT], BF); qtT_b = [cx.buf(f"qtT{i}") for i in range(GH)]
            ktT = cx.sb("ktT", [128, GH, TT], BF); ktT_b = [cx.buf(f"ktT{i}") for i in range(GH)]
            vT = cx.sb("vT", [128, TT], BF); vT_b = cx.buf("vT")
            v_tm = cx.sb("v_tm", [128, GH, NSUB, 128], BF); v_tm_b = [cx.buf(f"v_tm{i}") for i in range(GH)]
            k_tm = cx.sb("k_tm", [128, GH, NSUB, 128], BF); k_tm_b = [cx.buf(f"k_tm{i}") for i in range(GH)]
            sog = cx.sb("sog", [128, GH, TT], BF); sog_b = [cx.buf(f"sog{i}") for i in range(GH)]
            osb = cx.sb("osb", [128, GH, TT], F32); osb_b = [cx.buf(f"osb{i}") for i in range(GH)]
            lat, lat_b = osb, osb_b
            sqo = cx.sb("sqo", [128, GH, TT], BF); sqo_b = [cx.buf(f"sqo{i}") for i in range(GH)]
            rstd = cx.sb("rstd", [128, TT], F32); rstd_b = cx.buf("rstd")
            AT = [cx.sb(f"AT{i}", [128, 128], BF) for i in range(2)]; AT_b = [cx.buf(f"AT{i}") for i in range(2)]
            state = cx.sb("state", [128, HG_H, 128], F32); state_b = [cx.buf(f"state{i}") for i in range(HG_H)]
            stbf = cx.sb("stbf", [128, GH, 128], BF); stbf_b = [cx.buf(f"stbf{i}") for i in range(GH)]
            kvt = cx.sb("kvt", [128, GH, 128], F32); kvt_b = [cx.buf(f"kvt{i}") for i in range(GH)]
            ev = cx.sb("ev", [128, GH, 3, 2 * NSUB], F32); ev_b = [cx.buf(f"ev{i}") for i in range(GH)]
            oaT = cx.sb("oaT", [128, HG_H, TT], BF); oaT_b = [cx.buf(f"oaT{i}") for i in range(HG_H)]
            ga = cx.sb("ga", [128, TT], F32); ga_b = cx.buf("ga")
            yag = [cx.sb(f"yag{i}", [128, TT], BF) for i in range(2)]; yag_b = [cx.buf(f"yag{i}") for i in range(2)]
            latn = cx.sb("latn", [128, 4, TT], BF); latn_b = cx.buf("latn")
            cst = cx.sb("cst", [64, 2, TT], F32); cst_b = cx.buf("cst")
            kp = cx.sb("kp", [64, 2, TT], F32); kp_b = cx.buf("kp")
            krt = cx.sb("krt", [64, TT], BF); krt_b = cx.buf("krt")
            wrot = cx.sb("wrot", [128, KC, 64], BF); wrot_b = cx.buf("wrot")
            for h in range(HG_H):
                cx.op("pool", lambda e: e.memset(state[:, h, :], 0.0), writes=[state_b[h]])
            hT, hT_b = ph.hT, ph.hT_b
            rhs = lambda k: hT[:, k, :]

            for T in range(NT):
                t0 = T * TT
                ph.prenorm(xin, xin_bufs, T)
                for g in range(HG_H // GH):
                    def f_cb(ci, pb):
                        hl = ci; h = g * GH + hl
                        sgm, sgm_b = f32t["sgm"]; lf, lf_b = f32t["lf"]; bb, bb_b = f32t["bb"]; bp, bp_b = f32t["bp"]
                        eb, eb_b = f32t["eb"]; enb, enb_b = f32t["enb"]
                        cx.op("act", lambda e: e.activation(out=sgm[:], in_=self.ps[pb][:], func=AF.Sigmoid),
                              reads=[self.psb[pb]], writes=[sgm_b])
                        cx.op("act", lambda e: e.activation(out=lf[:], in_=sgm[:], func=AF.Ln, scale=oml[:, h:h + 1],
                                                            bias=lb[:, h:h + 1]), reads=[sgm_b, lb_b], writes=[lf_b])
                        cx.op("dve", lambda e: e.tensor_tensor_scan(out=bb[:], data0=rmask[:], data1=lf[:], initial=0.0,
                                                                    op0=ALU.mult, op1=ALU.add),
                              reads=[rmask_b, lf_b], writes=[bb_b])
                        b3 = bb[:].rearrange("p (a b) -> p a b", b=64)
                        cx.op("dve", lambda e: e.tensor_tensor(out=bp[:].rearrange("p (a b) -> p a b", b=64), in0=b3,
                                                               in1=b3[:, :, 31:32].to_broadcast([128, 2 * NSUB, 64]),
                                                               op=ALU.subtract), reads=[bb_b], writes=[bp_b])
                        cx.op("dve", lambda e: e.tensor_scalar(out=bp[:], in0=bp[:], scalar1=40.0, scalar2=-40.0,
                                                               op0=ALU.min, op1=ALU.max), reads=[bp_b], writes=[bp_b])
                        cx.op("act", lambda e: e.activation(out=eb[:], in_=bp[:], func=AF.Exp), reads=[bp_b], writes=[eb_b])
                        cx.op("act", lambda e: e.activation(out=enb[:], in_=bp[:], func=AF.Exp, scale=-1.0),
                              reads=[bp_b], writes=[enb_b])
                        cx.op("act", lambda e: e.activation(out=ev[:, hl, 0, :], in_=b3[:, :, 63], func=AF.Exp),
                              reads=[bb_b], writes=[ev_b[hl]])
                        cx.op("act", lambda e: e.activation(out=ev[:, hl, 1, :], in_=b3[:, :, 31], func=AF.Exp),
                              reads=[bb_b], writes=[ev_b[hl]], accum=True)
                        cx.op("act", lambda e: e.activation(out=ev[:, hl, 2, :],
                                                            in_=bp[:].rearrange("p (a b) -> p a b", b=64)[:, :, 63],
                                                            func=AF.Exp), reads=[bp_b], writes=[ev_b[hl]], accum=True)
                        cx.op("dve", lambda e: e.tensor_scalar(out=sgm[:], in0=sgm[:], scalar1=noml[:, h:h + 1],
                                                               scalar2=oml[:, h:h + 1], op0=ALU.mult, op1=ALU.add),
                              reads=[sgm_b, lb_b], writes=[sgm_b])
                        cx.op("dve", lambda e: e.tensor_tensor(out=ktT[:, hl, :], in0=sgm[:], in1=enb[:], op=ALU.mult),
                              reads=[sgm_b, enb_b], writes=[ktT_b[hl]])
                        cx.op("pool", lambda e: e.tensor_copy(out=qtT[:, hl, :], in_=eb[:]), reads=[eb_b],
                              writes=[qtT_b[hl]])
                        self.transpose_to(ktT[:, hl, :], ktT_b[hl], NSUB,
                                          lambda c0, n: k_tm[:, hl, c0:c0 + n, :], k_tm_b[hl])
                    ph.proj_fm(W, OFF_F + g * GH * 128, GH * 128, rhs, hT_b, f_cb)

                    def q_cb(ci, pb):
                        hl = ci
                        sq, sq_b = f32t["lf"]
                        cx.op("act", lambda e: e.activation(out=sq[:], in_=self.ps[pb][:], func=AF.Silu),
                              reads=[self.psb[pb]], writes=[sq_b])
                        cx.op("dve", lambda e: e.tensor_tensor(out=qtT[:, hl, :], in0=sq[:], in1=qtT[:, hl, :], op=ALU.mult),
                              reads=[sq_b, qtT_b[hl]], writes=[qtT_b[hl]])
                    ph.proj_fm(W, OFF_Q + g * GH * 128, GH * 128, rhs, hT_b, q_cb)

                    def i_cb(ci, pb):
                        hl = ci
                        cx.op("act", lambda e: e.copy(out=vT[:], in_=self.ps[pb][:]), reads=[self.psb[pb]], writes=[vT_b])
                        self.transpose_to(vT, vT_b, NSUB, lambda c0, n: v_tm[:, hl, c0:c0 + n, :], v_tm_b[hl])
                    ph.proj_fm(W, OFF_I + g * GH * 128, GH * 128, rhs, hT_b, i_cb)

                    def og_cb(ci, pb):
                        hl = ci
                        cx.op("act", lambda e: e.activation(out=sog[:, hl, :], in_=self.ps[pb][:], func=AF.Silu),
                              reads=[self.psb[pb]], writes=[sog_b[hl]])
                    ph.proj_fm(W, OFF_OG + g * GH * 128, GH * 128, rhs, hT_b, og_cb)

                    for i in range(NSUB):
                        tsl = slice(i * 128, (i + 1) * 128)
                        for hl in range(GH):
                            h = g * GH + hl
                            pa, po = ph.bank(), ph.bank()
                            cx.op("pe", lambda e: e.matmul(self.ps[pa][:, 0:128], lhsT=ktT[:, hl, tsl], rhs=qtT[:, hl, tsl],
                                                           start=True, stop=True),
                                  reads=[ktT_b[hl], qtT_b[hl]], writes=[self.psb[pa]])
                            at, atb = AT[(i * GH + hl) % 2], AT_b[(i * GH + hl) % 2]
                            cx.op("dve", lambda e: e.tensor_tensor(out=at[:], in0=self.ps[pa][:, 0:128], in1=self.tri[:],
                                                                   op=ALU.mult),
                                  reads=[self.psb[pa], self.tri_b], writes=[atb])
                            cx.op("pe", lambda e: e.matmul(self.ps[po][:, 0:128], lhsT=v_tm[:, hl, i, :], rhs=at[:],
                                                           start=True, stop=False),
                                  reads=[v_tm_b[hl], atb], writes=[self.psb[po]], inc=False)
                            for j in range(2):
                                c = 2 * i + j
                                csl = slice(i * 128 + j * 64, i * 128 + (j + 1) * 64)
                                psl = slice(j * 64, (j + 1) * 64)
                                cx.op("dve", lambda e: e.tensor_scalar(out=stbf[:, hl, :], in0=state[:, h, :],
                                                                       scalar1=ev[:, hl, 1, c:c + 1], scalar2=None,
                                                                       op0=ALU.mult),
                                      reads=[state_b[h], ev_b[hl]], writes=[stbf_b[hl]])
                                cx.op("pe", lambda e: e.matmul(self.ps[po][:, j * 64:(j + 1) * 64], lhsT=stbf[:, hl, :],
                                                               rhs=qtT[:, hl, csl], start=False, stop=(j == 1)),
                                      reads=[stbf_b[hl], qtT_b[hl]], writes=[self.psb[po]], accum=True)
                                pk = ph.bank()
                                while pk in (pa, po):
                                    pk = ph.bank()
                                cx.op("pe", lambda e: e.matmul(self.ps[pk][:, 0:128], lhsT=k_tm[psl, hl, i, :],
                                                               rhs=v_tm[psl, hl, i, :], start=True, stop=True),
                                      reads=[k_tm_b[hl], v_tm_b[hl]], writes=[self.psb[pk]])
                                cx.op("dve", lambda e: e.tensor_scalar(out=kvt[:, hl, :], in0=self.ps[pk][:, 0:128],
                                                                       scalar1=ev[:, hl, 2, c:c + 1], scalar2=None,
                                                                       op0=ALU.mult),
                                      reads=[self.psb[pk], ev_b[hl]], writes=[kvt_b[hl]])
                                cx.op("dve", lambda e: e.scalar_tensor_tensor(out=state[:, h, :], in0=state[:, h, :],
                                                                              scalar=ev[:, hl, 0, c:c + 1], in1=kvt[:, hl, :],
                                                                              op0=ALU.mult, op1=ALU.add),
                                      reads=[state_b[h], ev_b[hl], kvt_b[hl]], writes=[state_b[h]])
                            cx.op("act", lambda e: e.copy(out=osb[:, hl, tsl], in_=self.ps[po][:, 0:128]),
                                  reads=[self.psb[po]], writes=[osb_b[hl]], accum=(i > 0))
                    for hl in range(GH):
                        h = g * GH + hl
                        cx.op("act", lambda e: e.activation(out=sqo[:, hl, :], in_=osb[:, hl, :], func=AF.Square),
                              reads=[osb_b[hl]], writes=[sqo_b[hl]])
                        pn = ph.bank()
                        cx.op("pe", lambda e: e.matmul(self.ps[pn][:], lhsT=self.ones[:], rhs=sqo[:, hl, :],
                                                       start=True, stop=True),
                              reads=[self.ones_b, sqo_b[hl]], writes=[self.psb[pn]])
                        cx.op("act", lambda e: e.activation(out=rstd[:], in_=self.ps[pn][:], func=AF.Sqrt,
                                                            scale=1.0 / 128, bias=self.eps_ap(1.0)),
                              reads=[self.psb[pn]], writes=[rstd_b])
                        cx.op("dve", lambda e: e.reciprocal(out=rstd[:], in_=rstd[:]), reads=[rstd_b], writes=[rstd_b])
                        cx.op("dve", lambda e: e.scalar_tensor_tensor(out=osb[:, hl, :], in0=osb[:, hl, :],
                                                                      scalar=hgT[:, h:h + 1], in1=rstd[:],
                                                                      op0=ALU.mult, op1=ALU.mult),
                              reads=[osb_b[hl], hgT_b, rstd_b], writes=[osb_b[hl]])
                        cx.op("dve", lambda e: e.tensor_tensor(out=oaT[:, h, :], in0=osb[:, hl, :], in1=sog[:, hl, :],
                                                               op=ALU.mult),
                              reads=[osb_b[hl], sog_b[hl]], writes=[oaT_b[h]])
                for c in range(KC):
                    def ga_cb(ci, pb):
                        cx.op("act", lambda e: e.activation(out=ga[:], in_=self.ps[pb][:], func=AF.Sigmoid),
                              reads=[self.psb[pb]], writes=[ga_b])
                    ph.proj_fm(W, OFF_GA + c * 128, 128, rhs, hT_b, ga_cb)

                    def ya_cb(ci, pb):
                        yt, ytb = yag[c % 2], yag_b[c % 2]
                        cx.op("dve", lambda e: e.tensor_tensor(out=yt[:], in0=self.ps[pb][:], in1=ga[:], op=ALU.mult),
                              reads=[self.psb[pb], ga_b], writes=[ytb])
                        cx.dma(self.YA[c * 128:(c + 1) * 128, t0:t0 + TT], yt[:], ytb, reads=[ytb], writes=[self.YA_b[T]],
                               partial=True)
                    ph.proj_fm(I["w_branch_a"], c * 128, 128, lambda k: oaT[:, k, :], oaT_b, ya_cb)
                def gb_cb(ci, pb):
                    yt, ytb = yag[ci % 2], yag_b[ci % 2]
                    cx.op("act", lambda e: e.activation(out=yt[:], in_=self.ps[pb][:], func=AF.Sigmoid),
                          reads=[self.psb[pb]], writes=[ytb])
                    cx.dma(self.GB[ci * 128:(ci + 1) * 128, t0:t0 + TT], yt[:], ytb, reads=[ytb], writes=[self.GB_b[T]],
                           partial=True)
                ph.proj_fm(W, OFF_GB, D, rhs, hT_b, gb_cb)
                for (off, gT, gTb, dst, dst_b) in ((OFF_CQ, qgT, qgT_b, self.CQ, self.CQ_b),
                                                  (OFF_CKV, kgT, kgT_b, self.CKV, self.CKV_b)):
                    def lat_cb(ci, pb):
                        cx.op("act", lambda e: e.copy(out=lat[:, ci, :], in_=self.ps[pb][:]), reads=[self.psb[pb]],
                              writes=[lat_b[ci]])
                        cx.op("act", lambda e: e.activation(out=sqo[:, ci, :], in_=lat[:, ci, :], func=AF.Square),
                              reads=[lat_b[ci]], writes=[sqo_b[ci]])
                    ph.proj_fm(W, off, 512, rhs, hT_b, lat_cb)
                    pn = ph.bank()
                    for ci in range(4):
                        cx.op("pe", lambda e: e.matmul(self.ps[pn][:], lhsT=self.ones[:], rhs=sqo[:, ci, :],
                                                       start=(ci == 0), stop=(ci == 3)),
                              reads=[self.ones_b, sqo_b[ci]], writes=[self.psb[pn]], inc=(ci == 3), accum=(ci > 0))
                    cx.op("act", lambda e: e.activation(out=rstd[:], in_=self.ps[pn][:], func=AF.Sqrt,
                                                        scale=1.0 / 512, bias=self.eps_ap(1.0)),
                          reads=[self.psb[pn]], writes=[rstd_b])
                    cx.op("dve", lambda e: e.reciprocal(out=rstd[:], in_=rstd[:]), reads=[rstd_b], writes=[rstd_b])
                    for ci in range(4):
                        cx.op("dve", lambda e: e.scalar_tensor_tensor(out=latn[:, ci, :], in0=lat[:, ci, :],
                                                                      scalar=gT[:, ci:ci + 1], in1=rstd[:],
                                                                      op0=ALU.mult, op1=ALU.mult),
                              reads=[lat_b[ci], gTb, rstd_b], writes=[latn_b], accum=(ci > 0))
                    cx.dma(dst[:, t0:t0 + TT].rearrange("(c p) t -> p c t", p=128), latn[:], latn_b, reads=[latn_b],
                           writes=[dst_b[T]])
                cx.dma(cst[:], self.CS[:, :, t0:t0 + TT].rearrange("w p t -> p w t"), cst_b, reads=[self.CS_b],
                       writes=[cst_b])
                wv, wb = ph.load_wcols(W, OFF_KPE, 64)
                cx.op("act", lambda e: e.mul(out=wrot[:, :, 0:32], in_=wv[:, :, 32:64], mul=-1.0), reads=[wb],
                      writes=[wrot_b])
                cx.op("act", lambda e: e.copy(out=wrot[:, :, 32:64], in_=wv[:, :, 0:32]), reads=[wb], writes=[wrot_b],
                      accum=True)
                for wi, (wt, wtb) in enumerate(((wv, wb), (wrot, wrot_b))):
                    pb = ph.bank()
                    for k in range(KC):
                        cx.op("pe", lambda e: e.matmul(self.ps[pb][0:64, :], lhsT=wt[:, k, 0:64], rhs=hT[:, k, :],
                                                       start=(k == 0), stop=(k == KC - 1)),
                              reads=[wtb] + hT_b, writes=[self.psb[pb]], inc=(k == KC - 1), accum=(k > 0))
                    cx.op("dve", lambda e: e.tensor_tensor(out=kp[:, wi, :], in0=self.ps[pb][0:64, :], in1=cst[:, wi, :],
                                                           op=ALU.mult),
                          reads=[self.psb[pb], cst_b], writes=[kp_b], accum=(wi > 0))
                cx.op("dve", lambda e: e.tensor_tensor(out=krt[:], in0=kp[:, 0, :], in1=kp[:, 1, :], op=ALU.add),
                      reads=[kp_b], writes=[krt_b])
                cx.dma(self.KR[:, t0:t0 + TT], krt[:], krt_b, reads=[krt_b], writes=[self.KR_b[T]])
            cx.end_phase(mark)
            cx.st = old_st

    def mla_phase(self):
        cx, I = self.cx, self.I
        SC = 192 ** -0.5
        with contextlib.ExitStack() as st:
            old_st = cx.st; mark = len(cx.dma_sems)
            ph = PH(self, st, None, None, xy=False, wcols=64)
            cqT = cx.sb("cqT", [128, 4, S], BF); cq_b = cx.buf("cqT")
            ckT = cx.sb("ckT", [128, 4, S], BF); ck_b = cx.buf("ckT")
            krT = cx.sb("krT", [64, S], BF); kr_b = cx.buf("krT")
            cs = cx.sb("cs", [64, 2, S], F32); cs_b = cx.buf("cs")
            cx.dma(cqT[:], self.CQ.rearrange("(c p) t -> p c t", p=128), cq_b, reads=self.CQ_b, writes=[cq_b])
            cx.dma(ckT[:], self.CKV.rearrange("(c p) t -> p c t", p=128), ck_b, reads=self.CKV_b, writes=[ck_b])
            cx.dma(krT[:], self.KR, kr_b, reads=self.KR_b, writes=[kr_b])
            cx.dma(cs[:], self.CS.rearrange("w p t -> p w t"), cs_b, reads=[self.CS_b], writes=[cs_b])
            knT = cx.sb("knT", [128, S], BF); kn_b = cx.buf("knT")
            qnT = cx.sb("qnT", [128, S], BF); qn_b = cx.buf("qnT")
            qrT = cx.sb("qrT", [64, S], BF); qr_b = cx.buf("qrT")
            vtm = cx.sb("vtm", [128, S // 128, 128], BF); vt_b = cx.buf("vtm")
            wrot = cx.sb("wrotq", [128, 4, 64], BF); wrot_b = cx.buf("wrotq")
            qp = cx.sb("qp", [64, 2, TT], F32); qp_b = cx.buf("qp")
            PT = [cx.sb(f"PT{i}", [128, TT], BF) for i in range(2)]; PT_b = [cx.buf(f"PT{i}") for i in range(2)]
            rec = cx.sb("rec", [128, TT], F32); rec_b = cx.buf("rec")
            obt = [cx.sb(f"obt{i}", [128, TT], BF) for i in range(2)]; obt_b = [cx.buf(f"obt{i}") for i in range(2)]
            Wq, Wkv = I["mla_w_q_up"], I["mla_w_kv_up"]
            it = 0
            for h in range(MLA_H):
                wq, wqb = ph.load_w(Wq[:, h * 192:(h + 1) * 192].rearrange("(k p) n -> p k n", p=128), 4, 192)
                cx.op("act", lambda e: e.mul(out=wrot[:, :, 0:32], in_=wq[:, :, 160:192], mul=-1.0), reads=[wqb],
                      writes=[wrot_b])
                cx.op("act", lambda e: e.copy(out=wrot[:, :, 32:64], in_=wq[:, :, 128:160]), reads=[wqb], writes=[wrot_b],
                      accum=True)
                wk, wkb = ph.load_w(Wkv[:, h * 256:(h + 1) * 256].rearrange("(k p) n -> p k n", p=128), 4, 256)
                for tq in range(NT):
                    tsl = slice(tq * TT, (tq + 1) * TT)
                    for (wt, wtb, c0, src, srcb, dstT, dstb) in ((wk, wkb, 0, ckT, ck_b, knT, kn_b),
                                                                 (wq, wqb, 0, cqT, cq_b, qnT, qn_b)):
                        pb = ph.bank()
                        for k in range(4):
                            cx.op("pe", lambda e: e.matmul(self.ps[pb][:], lhsT=wt[:, k, c0:c0 + 128], rhs=src[:, k, tsl],
                                                           start=(k == 0), stop=(k == 3)),
                                  reads=[wtb, srcb], writes=[self.psb[pb]], inc=(k == 3), accum=(k > 0))
                        cx.op("act", lambda e: e.copy(out=dstT[:, tsl], in_=self.ps[pb][:]), reads=[self.psb[pb]],
                              writes=[dstb], accum=(tq > 0))
                    for wi, (wt, wtb, c0) in enumerate(((wq, wqb, 128), (wrot, wrot_b, 0))):
                        pb = ph.bank()
                        for k in range(4):
                            cx.op("pe", lambda e: e.matmul(self.ps[pb][0:64, :], lhsT=wt[:, k, c0:c0 + 64], rhs=cqT[:, k, tsl],
                                                           start=(k == 0), stop=(k == 3)),
                                  reads=[wtb, cq_b], writes=[self.psb[pb]], inc=(k == 3), accum=(k > 0))
                        cx.op("dve", lambda e: e.tensor_tensor(out=qp[:, wi, :], in0=self.ps[pb][0:64, :], in1=cs[:, wi, tsl],
                                                               op=ALU.mult),
                              reads=[self.psb[pb], cs_b], writes=[qp_b], accum=(wi > 0))
                    cx.op("dve", lambda e: e.tensor_tensor(out=qrT[:, tsl], in0=qp[:, 0, :], in1=qp[:, 1, :], op=ALU.add),
                          reads=[qp_b], writes=[qr_b], accum=(tq > 0))
                    pb = ph.bank()
                    for kk in range(4):
                        kt = tq * 4 + kk
                        for k in range(4):
                            cx.op("pe", lambda e: e.matmul(self.ps[pb][:, kk * 128:(kk + 1) * 128],
                                                           lhsT=ckT[:, k, kt * 128:(kt + 1) * 128], rhs=wk[:, k, 128:256],
                                                           start=(k == 0), stop=(k == 3), skip_group_check=True),
                                  reads=[wkb, ck_b], writes=[self.psb[pb]], inc=(k == 3 and kk == 3),
                                  accum=not (k == 0 and kk == 0))
                    cx.op("act", lambda e: e.copy(out=vtm[:, tq * 4:(tq + 1) * 4, :],
                                                  in_=self.ps[pb][:].rearrange("p (a b) -> p a b", b=128)),
                          reads=[self.psb[pb]], writes=[vt_b], accum=(tq > 0))
                for Q in range(NT):
                    po, pd = ph.bank(), ph.bank()
                    nkt = 4 * (Q + 1)
                    for kt in range(nkt):
                        d = kt - 4 * Q
                        c0 = max(d, 0) * 128
                        q0 = Q * TT + c0
                        ncol = TT - c0
                        pS = ph.bank()
                        while pS in (po, pd):
                            pS = ph.bank()
                        cx.op("pe", lambda e: e.matmul(self.ps[pS][:, 0:ncol], lhsT=knT[:, kt * 128:(kt + 1) * 128],
                                                       rhs=qnT[:, q0:q0 + ncol], start=True, stop=False),
                              reads=[kn_b, qn_b], writes=[self.psb[pS]], inc=False)
                        cx.op("pe", lambda e: e.matmul(self.ps[pS][:, 0:ncol], lhsT=krT[:, kt * 128:(kt + 1) * 128],
                                                       rhs=qrT[:, q0:q0 + ncol], start=False, stop=True),
                              reads=[kr_b, qr_b], writes=[self.psb[pS]], accum=True)
                        pt, ptb = PT[it % 2], PT_b[it % 2]
                        it += 1
                        cx.op("act", lambda e: e.activation(out=pt[:, 0:ncol], in_=self.ps[pS][:, 0:ncol], func=AF.Exp,
                                                            scale=SC), reads=[self.psb[pS]], writes=[ptb])
                        if d >= 0:
                            cx.op("dve", lambda e: e.tensor_tensor(out=pt[:, 0:128], in0=pt[:, 0:128], in1=self.cmask[:],
                                                                   op=ALU.mult), reads=[ptb, self.cmask_b], writes=[ptb])
                        first, last = (kt == 0), (kt == nkt - 1)
                        cx.op("pe", lambda e: e.matmul(self.ps[po][:, c0:TT], lhsT=vtm[:, kt, :], rhs=pt[:, 0:ncol],
                                                       start=first, stop=last),
                              reads=[vt_b, ptb], writes=[self.psb[po]], inc=last, accum=not first)
                        cx.op("pe", lambda e: e.matmul(self.ps[pd][:, c0:TT], lhsT=self.ones[:], rhs=pt[:, 0:ncol],
                                                       start=first, stop=last),
                              reads=[self.ones_b, ptb], writes=[self.psb[pd]], inc=True, accum=not first)
                    cx.op("dve", lambda e: e.reciprocal(out=rec[:], in_=self.ps[pd][:]), reads=[self.psb[pd]], writes=[rec_b])
                    ot, otb = obt[Q % 2], obt_b[Q % 2]
                    cx.op("dve", lambda e: e.tensor_tensor(out=ot[:], in0=self.ps[po][:], in1=rec[:], op=ALU.mult),
                          reads=[self.psb[po], rec_b], writes=[otb])
                    cx.dma(self.OB[h * 128:(h + 1) * 128, Q * TT:(Q + 1) * TT], ot[:], otb, reads=[otb],
                           writes=[self.OB_b[Q]], partial=True)
            cx.end_phase(mark)
            cx.st = old_st

    def merge_phase(self, xres, xres_bufs, xout, xout_bufs):
        cx, I = self.cx, self.I
        with contextlib.ExitStack() as st:
            old_st = cx.st; mark = len(cx.dma_sems)
            ph = PH(self, st, None, I["mix_post_g"])
            obT = cx.sb("obT", [128, KC, TT], BF); obT_b = cx.buf("obT")
            yT = cx.sb("yT", [128, KC, TT], BF); yT_b = [cx.buf(f"yT{i}") for i in range(KC)]
            gy = [cx.sb(f"gy{i}", [128, 2, TT], BF) for i in range(2)]; gy_b = [cx.buf(f"gy{i}") for i in range(2)]
            tmp = cx.sb("mtmp", [128, TT], F32); tmp_b = cx.buf("mtmp")
            for T in range(NT):
                t0 = T * TT
                cx.dma(obT[:], self.OB[:, t0:t0 + TT].rearrange("(c p) t -> p c t", p=128), obT_b, reads=[self.OB_b[T]],
                       writes=[obT_b])

                def yb_cb(ci, pb):
                    g_, g_b = gy[ci % 2], gy_b[ci % 2]
                    cx.dma(g_[:, 0, :], self.GB[ci * 128:(ci + 1) * 128, t0:t0 + TT], g_b, reads=[self.GB_b[T]], writes=[g_b])
                    cx.dma(g_[:, 1, :], self.YA[ci * 128:(ci + 1) * 128, t0:t0 + TT], g_b, reads=[self.YA_b[T]], writes=[g_b],
                           partial=True)
                    cx.op("dve", lambda e: e.tensor_tensor(out=tmp[:], in0=self.ps[pb][:], in1=g_[:, 0, :], op=ALU.mult),
                          reads=[self.psb[pb], g_b], writes=[tmp_b])
                    cx.op("dve", lambda e: e.tensor_tensor(out=yT[:, ci, :], in0=tmp[:], in1=g_[:, 1, :], op=ALU.add),
                          reads=[tmp_b, g_b], writes=[yT_b[ci]])
                ph.proj_fm(I["w_branch_b"], 0, D, lambda k: obT[:, k, :], [obT_b], yb_cb)
                ph.tm_proj_post(lambda j, s: yT[:, j, s * 128:(s + 1) * 128], lambda j: yT_b[j], KC, I["w_out"], T,
                                xres, xres_bufs, xout, xout_bufs, 1.0)
            cx.end_phase(mark)
            cx.st = old_st

    def xa_phase(self, xin, xin_bufs, xout, xout_bufs):
        cx, I = self.cx, self.I
        SC = 128 ** -0.5
        with contextlib.ExitStack() as st:
            old_st = cx.st; mark = len(cx.dma_sems)
            ph = PH(self, st, I["xa_pre_g"], I["xa_post_g"])
            gm = ph.xy[:, 0, :]
            gm_b = ph.xy_b[0]
            cx.dma(gm, I["xa_mem_g"].partition_broadcast(128), gm_b, writes=[gm_b])
            mT = cx.sb("mT", [128, KC, N_MEM], BF); mT_b = cx.buf("mT")
            kmT = cx.sb("kmT", [128, XA_H, N_MEM], BF); km_b = cx.buf("kmT")
            vm = cx.sb("vm", [128, 2, 512], BF); vm_b = cx.buf("vm")
            qT = cx.sb("qT", [128, XA_H, TT], BF); qT_b = [cx.buf(f"qT{i}") for i in range(XA_H)]
            oT = cx.sb("oT", [128, XA_H, TT], BF); oT_b = [cx.buf(f"oT{i}") for i in range(XA_H)]
            PT = [cx.sb(f"PTx{i}", [128, TT], BF) for i in range(2)]; PT_b = [cx.buf(f"PTx{i}") for i in range(2)]
            rec = cx.sb("recx", [128, TT], F32); rec_b = cx.buf("recx")
            for s in range(N_MEM // 128):
                xt, xb = ph.xr[s % 2], ph.xr_b[s % 2]
                cx.dma(xt[:], I["mem"][s * 128:(s + 1) * 128, :], xb, writes=[xb])
                hb, hbb = ph.h0[s % 2], ph.h0_b[s % 2]
                junk, junk_b = ph.nextjunk()
                cx.op("act", lambda e: e.activation(out=junk[:], in_=xt[:], func=AF.Square, accum_out=ph.ssp[:, s:s + 1]),
                      reads=[xb], writes=[junk_b, ph.ssp_b])
                self.rstd_from_ss(ph.ssp[:, s:s + 1], ph.rs[:, s:s + 1], D, [ph.ssp_b], [ph.rs_b])
                cx.op("dve", lambda e: e.scalar_tensor_tensor(out=hb[:], in0=xt[:], scalar=ph.rs[:, s:s + 1], in1=gm,
                                                              op0=ALU.mult, op1=ALU.mult),
                      reads=[xb, ph.rs_b, gm_b], writes=[hbb])
                self.transpose_to(hb, hbb, KC, lambda c0, n: mT[:, c0:c0 + n, s * 128:(s + 1) * 128], mT_b)
                mT_b.w = list(mT_b.w)

            def km_cb(ci, pb):
                cx.op("act", lambda e: e.copy(out=kmT[:, ci, :], in_=self.ps[pb][:, 0:N_MEM]), reads=[self.psb[pb]],
                      writes=[km_b], accum=(ci > 0))
            ph.proj_fm(I["xa_w_k"], 0, 512, lambda k: mT[:, k, :], [mT_b], km_cb, nfree=N_MEM)
            wv_, wvb = ph.load_wcols(I["xa_w_v"], 0, 256)
            wv2, wvb2 = ph.load_wcols(I["xa_w_v"], 256, 256)
            for kt in range(2):
                pb = ph.bank()
                for hf, (wt, wtb) in enumerate(((wv_, wvb), (wv2, wvb2))):
                    for k in range(KC):
                        cx.op("pe", lambda e: e.matmul(self.ps[pb][:, hf * 256:(hf + 1) * 256],
                                                       lhsT=mT[:, k, kt * 128:(kt + 1) * 128], rhs=wt[:, k, :],
                                                       start=(k == 0), stop=(k == KC - 1), skip_group_check=True),
                              reads=[wtb, mT_b], writes=[self.psb[pb]], inc=(k == KC - 1 and hf == 1),
                              accum=not (k == 0 and hf == 0))
                cx.op("act", lambda e: e.copy(out=vm[:, kt, :], in_=self.ps[pb][:]), reads=[self.psb[pb]], writes=[vm_b],
                      accum=(kt > 0))
            it = 0
            for T in range(NT):
                ph.prenorm(xin, xin_bufs, T)

                def q_cb(ci, pb):
                    cx.op("act", lambda e: e.copy(out=qT[:, ci, :], in_=self.ps[pb][:]), reads=[self.psb[pb]],
                          writes=[qT_b[ci]])
                ph.proj_fm(I["xa_w_q"], 0, 512, lambda k: ph.hT[:, k, :], ph.hT_b, q_cb)
                for hh in range(XA_H):
                    po, pd = ph.bank(), ph.bank()
                    for kt in range(2):
                        pS = ph.bank()
                        cx.op("pe", lambda e: e.matmul(self.ps[pS][:], lhsT=kmT[:, hh, kt * 128:(kt + 1) * 128],
                                                       rhs=qT[:, hh, :], start=True, stop=True),
                              reads=[km_b, qT_b[hh]], writes=[self.psb[pS]])
                        pt, ptb = PT[it % 2], PT_b[it % 2]
                        it += 1
                        cx.op("act", lambda e: e.activation(out=pt[:], in_=self.ps[pS][:], func=AF.Exp, scale=SC),
                              reads=[self.psb[pS]], writes=[ptb])
                        cx.op("pe", lambda e: e.matmul(self.ps[po][:], lhsT=vm[:, kt, hh * 128:(hh + 1) * 128], rhs=pt[:],
                                                       start=(kt == 0), stop=(kt == 1)),
                              reads=[vm_b, ptb], writes=[self.psb[po]], accum=(kt > 0))
                        cx.op("pe", lambda e: e.matmul(self.ps[pd][:], lhsT=self.ones[:], rhs=pt[:],
                                                       start=(kt == 0), stop=(kt == 1)),
                              reads=[self.ones_b, ptb], writes=[self.psb[pd]], accum=(kt > 0))
                    cx.op("dve", lambda e: e.reciprocal(out=rec[:], in_=self.ps[pd][:]), reads=[self.psb[pd]], writes=[rec_b])
                    cx.op("dve", lambda e: e.tensor_tensor(out=oT[:, hh, :], in0=self.ps[po][:], in1=rec[:], op=ALU.mult),
                          reads=[self.psb[po], rec_b], writes=[oT_b[hh]])
                ph.tm_proj_post(lambda j, s: oT[:, j, s * 128:(s + 1) * 128], lambda j: oT_b[j], XA_H, I["xa_w_o"], T,
                                xin, xin_bufs, xout, xout_bufs, 1.0)
            cx.end_phase(mark)
            cx.st = old_st


_CACHE = {}


def _consts():
    idx = np.arange(128)
    tri = ((idx[:, None] <= idx[None, :]) & ((idx[:, None] // 64) == (idx[None, :] // 64))).astype(np.float32)
    cm = np.ones((128, 128), np.float32)
    cm[64:, :64] = 0.0
    invf = (1.0 / (np.float32(10000.0) ** (np.arange(0, 64, 2, dtype=np.float32) / np.float32(64)))).astype(np.float32)
    invf = np.concatenate([invf, invf]).reshape(64, 1).astype(np.float32)
    return {"ident_in": np.eye(128, dtype=np.float32), "tri_in": tri, "cmask_in": cm, "invf_in": invf}


def kernel(**inputs):
    inputs = {k: np.asarray(v) for k, v in inputs.items()}
    if "nc" not in _CACHE:
        _CACHE["nc"] = Prog().build()
    nc = _CACHE["nc"]
    shared = dict(_consts())
    for k, v in inputs.items():
        if k in ("x", "mem", "positions"):
            continue
        if k == "hgrn_lb_logits":
            shared[k] = np.ascontiguousarray(v)
        elif v.ndim == 2:
            shared[k] = np.ascontiguousarray(v[0:1])
        else:
            shared[k] = np.ascontiguousarray(v[0])
    in_maps = []
    for c in range(8):
        m = dict(shared)
        m["x"] = np.ascontiguousarray(inputs["x"][c])
        m["mem"] = np.ascontiguousarray(inputs["mem"][c])
        m["positions"] = np.ascontiguousarray(inputs["positions"][c:c + 1]).astype(np.int32)
        in_maps.append(m)
    res = run_bass_kernel_spmd(nc, in_maps, core_ids=list(range(8)))
    out = np.stack([np.asarray(r["out"]) for r in res.results], axis=0)
    return out.astype(np.float32)
```

```python
import contextlib
import numpy as np
import concourse.bass as bass
import concourse.mybir as mybir
from concourse.bass_utils import run_bass_kernel_spmd

F32 = mybir.dt.float32
BF = mybir.dt.bfloat16
I32 = mybir.dt.int32
AF = mybir.ActivationFunctionType
ALU = mybir.AluOpType

D = 2048
S = 2048
DFF = 5632
TT = 512
NT = S // TT
NSUB = TT // 128
KC = D // 128
EPS = 1e-6
IN_DIM = 13376
N_MEM = 256

SAME_ENG_SYNC = True
OPT_PSUM_STT = False
OPT_ACT_RSTD = False
OPT_WSCRATCH = True


class Eng:
    def __init__(self, name, h, sem):
        self.name, self.h, self.sem = name, h, sem
        self.count = 0
        self.waited = {}


class Buf:
    __slots__ = ("name", "w", "r", "sem", "dcount")

    def __init__(self, name):
        self.name = name
        self.w = []
        self.r = []
        self.sem = None
        self.dcount = 0


class Cx:
    def __init__(self, nc, st):
        self.nc, self.st = nc, st
        self.E = {}
        for name, h in (("pe", nc.tensor), ("act", nc.scalar), ("dve", nc.vector),
                        ("pool", nc.gpsimd), ("sp", nc.sync)):
            self.E[name] = Eng(name, h, nc.alloc_semaphore(name="sem_" + name))
        self.dma_sems = []
        self.sem_pool = []
        self.nsem = 0
        self.nbuf = 0

    def buf(self, name=None):
        self.nbuf += 1
        return Buf(f"{name or 'b'}_{self.nbuf}")

    def end_phase(self, mark):
        self.barrier()
        for b in self.dma_sems[mark:]:
            self.sem_pool.append((b.sem, b.dcount))
        del self.dma_sems[mark:]

    def sb(self, name, shape, dt):
        self.nbuf += 1
        return self.st.enter_context(self.nc.sbuf_tensor(f"{name}_{self.nbuf}", list(shape), dt))

    def _wait(self, E, dep):
        if dep[0] == "eng":
            X, need = dep[1], dep[2]
            if X is E:
                if E.name == "pe" or not SAME_ENG_SYNC or not dep[3]:
                    return
            assert need <= X.count, f"wait on not-yet-issued inc {X.name} {need}>{X.count}"
            key = X.name
            if E.waited.get(key, 0) >= need:
                return
            E.h.wait_ge(X.sem, need)
            E.waited[key] = need
        else:
            sem, val, key = dep[1], dep[2], dep[3]
            if E.waited.get(key, 0) >= val:
                return
            E.h.wait_ge(sem, val)
            E.waited[key] = val

    def _deps(self, E, reads, writes):
        for b in reads:
            for d in b.w:
                self._wait(E, d)
        for b in writes:
            for d in b.w:
                self._wait(E, d)
            for d in b.r:
                self._wait(E, d)

    def op(self, eng, fn, reads=(), writes=(), inc=True, accum=False):
        E = self.E[eng]
        self._deps(E, reads, writes)
        inst = fn(E.h)
        need = E.count + 1
        if inc:
            inst.then_inc(E.sem, 1)
            E.count += 1
        dep = ("eng", E, need, True)
        for b in reads:
            b.r.append(dep)
        for b in writes:
            if accum:
                b.w.append(dep)
            else:
                b.w = [dep]
                b.r = []
        return inst

    def dma(self, out_ap, in_ap, sbuf, reads=(), writes=(), q="sp", partial=False, **kw):
        E = self.E[q]
        self._deps(E, reads, writes)
        if sbuf.sem is None:
            if self.sem_pool:
                sbuf.sem, sbuf.dcount = self.sem_pool.pop()
            else:
                self.nsem += 1
                sbuf.sem = self.nc.alloc_semaphore(name=f"dsem{self.nsem}")
            self.dma_sems.append(sbuf)
        sbuf.dcount += 16
        E.h.dma_start(out=out_ap, in_=in_ap, **kw).then_inc(sbuf.sem, 16)
        dep = ("dma", sbuf.sem, sbuf.dcount, sbuf.name)
        for b in reads:
            b.r.append(dep)
        for b in writes:
            if partial:
                b.w.append(dep)
            else:
                b.w = [dep]
                b.r = []

    def barrier(self):
        for E in self.E.values():
            for X in self.E.values():
                if X is E or X.count == 0:
                    continue
                if E.waited.get(X.name, 0) < X.count:
                    E.h.wait_ge(X.sem, X.count)
                    E.waited[X.name] = X.count
            for b in self.dma_sems:
                if E.waited.get(b.name, 0) < b.dcount:
                    E.h.wait_ge(b.sem, b.dcount)
                    E.waited[b.name] = b.dcount


HG_H = 16
Q_LORA = 512
KV_LORA = 512
MLA_H = 16
XA_H = 4
TWO_PI = 6.283185307179586
C1 = 6.28125
C2 = TWO_PI - C1


class PH:
    def __init__(self, P, st, pre_g=None, post_g=None, xy=True, nst=2, nwb=3, wcols=256):
        self.P, self.cx = P, P.cx
        cx = self.cx
        cx.st = st
        self.wcols = wcols
        if pre_g is not None:
            self.gpre = cx.sb("gpre", [128, D], F32); self.gpre_b = cx.buf("gpre")
            cx.dma(self.gpre[:], pre_g.partition_broadcast(128), self.gpre_b, writes=[self.gpre_b])
            self.h0 = [cx.sb(f"h0_{i}", [128, D], BF) for i in range(2)]; self.h0_b = [cx.buf(f"h0_{i}") for i in range(2)]
            self.hT = cx.sb("hT", [128, KC, TT], BF); self.hT_b = [cx.buf(f"hT{s}") for s in range(NSUB)]
            self.ssp = cx.sb("ssp", [128, 4], F32); self.ssp_b = cx.buf("ssp")
        if post_g is not None:
            self.gpost = cx.sb("gpost", [128, D], F32); self.gpost_b = cx.buf("gpost")
            cx.dma(self.gpost[:], post_g.partition_broadcast(128), self.gpost_b, writes=[self.gpost_b])
            self.ss = cx.sb("ss", [128, 16], F32); self.ss_b = cx.buf("ss")
        if xy:
            self.xy = cx.sb("xy", [128, NSUB, D], F32); self.xy_b = [cx.buf(f"xy{s}") for s in range(NSUB)]
        self.xr = [cx.sb(f"xr{i}", [128, D], F32) for i in range(2)]; self.xr_b = [cx.buf(f"xr{i}") for i in range(2)]
        self.NST, self.NWB = nst, nwb
        self.wst = [cx.sb(f"wst{i}", [128, KC * wcols], F32) for i in range(nst)]; self.wst_b = [cx.buf(f"wst{i}") for i in range(nst)]
        self.wbf = [cx.sb(f"wbf{i}", [128, KC * wcols], BF) for i in range(nwb)]; self.wbf_b = [cx.buf(f"wbf{i}") for i in range(nwb)]
        self.junks = [cx.sb(f"junk{i}", [128, D], BF) for i in range(2)]; self.junk_bs = [cx.buf(f"junk{i}") for i in range(2)]
        self.rs = cx.sb("rs", [128, 8], F32); self.rs_b = cx.buf("rs"); self.rsq_b = cx.buf("rsq")
        self.jctr = 0; self.cast_rr = 0; self.wctr = 0; self.sctr = 0; self.pbank = 0
        self.ws = {}
        self.cast_engs = ["act", "dve", "pool"]

    def nextjunk(self):
        self.jctr += 1
        return self.junks[self.jctr % 2], self.junk_bs[self.jctr % 2]

    def bank(self):
        b = self.pbank % 8
        self.pbank += 1
        return b

    def cast(self, out_ap, in_ap, reads, writes):
        cx = self.cx
        eng = self.cast_engs[self.cast_rr % len(self.cast_engs)]
        self.cast_rr += 1
        if eng == "act":
            cx.op("act", lambda e: e.copy(out=out_ap, in_=in_ap), reads=reads, writes=writes)
        else:
            cx.op(eng, lambda e: e.tensor_copy(out=out_ap, in_=in_ap), reads=reads, writes=writes)

    def load_w(self, src_ap, k, n, key=None):
        cx = self.cx
        w_t, w_b = self.wbf[self.wctr % self.NWB], self.wbf_b[self.wctr % self.NWB]
        self.wctr += 1
        wview = w_t[:, 0:k * n].rearrange("p (k n) -> p k n", k=k)
        if key is not None and OPT_WSCRATCH and key in self.ws:
            d_ap, d_b = self.ws[key]
            cx.dma(w_t[:, 0:k * n], d_ap, w_b, reads=[d_b], writes=[w_b])
            return wview, w_b
        s_t, s_b = self.wst[self.sctr % self.NST], self.wst_b[self.sctr % self.NST]
        self.sctr += 1
        sview = s_t[:, 0:k * n].rearrange("p (k n) -> p k n", k=k)
        cx.dma(sview, src_ap, s_b, writes=[s_b])
        self.cast(wview, sview, [s_b], [w_b])
        if key is not None and OPT_WSCRATCH:
            self.P.nws += 1
            d_ap = self.P.dram_tmp(f"ws{self.P.nws}", [128, k * n], BF)
            d_b = cx.buf("ws")
            cx.dma(d_ap, w_t[:, 0:k * n], w_b, reads=[w_b], writes=[d_b])
            self.ws[key] = (d_ap, d_b)
        return wview, w_b

    def load_wcols(self, W, c0, n, kc=KC, key=None):
        return self.load_w(W[:, c0:c0 + n].rearrange("(k p) n -> p k n", p=128), kc, n, key=key)

    def prenorm(self, xin, xin_bufs, T):
        P, cx = self.P, self.cx
        t0 = T * TT
        for s in range(NSUB):
            r0 = t0 + s * 128
            xt, xb = self.xr[s % 2], self.xr_b[s % 2]
            cx.dma(xt[:], xin[r0:r0 + 128, :], xb, reads=[xin_bufs[T * NSUB + s]], writes=[xb])
            hb, hbb = self.h0[s % 2], self.h0_b[s % 2]
            junk, junk_b = self.nextjunk()
            cx.op("act", lambda e: e.activation(out=junk[:], in_=xt[:], func=AF.Square,
                                                accum_out=self.ssp[:, s:s + 1]),
                  reads=[xb], writes=[junk_b, self.ssp_b])
            P.rstd_from_ss(self.ssp[:, s:s + 1], self.rs[:, s:s + 1], D, [self.ssp_b], [self.rs_b])
            cx.op("dve", lambda e: e.scalar_tensor_tensor(out=hb[:], in0=xt[:], scalar=self.rs[:, s:s + 1],
                                                          in1=self.gpre[:], op0=ALU.mult, op1=ALU.mult),
                  reads=[xb, self.rs_b, self.gpre_b], writes=[hbb])
            P.transpose_to(hb, hbb, KC, lambda c0, n: self.hT[:, c0:c0 + n, s * 128:(s + 1) * 128], self.hT_b[s])

    def proj_fm(self, W, c0, ncols, rhs_fn, rhs_bufs, cb, kc=KC, m=128, nfree=TT):
        P, cx = self.P, self.cx
        done = 0
        while done < ncols:
            n = min(self.wcols, ncols - done)
            wv, wb = self.load_wcols(W, c0 + done, n, kc, key=(id(W), c0 + done, n))
            for jj in range(0, n, m):
                mm = min(m, n - jj)
                pb = self.bank()
                for k in range(kc):
                    cx.op("pe", lambda e: e.matmul(P.ps[pb][0:mm, 0:nfree], lhsT=wv[:, k, jj:jj + mm], rhs=rhs_fn(k),
                                                   start=(k == 0), stop=(k == kc - 1)),
                          reads=[wb] + list(rhs_bufs), writes=[P.psb[pb]], inc=(k == kc - 1), accum=(k > 0))
                cb((done + jj) // m, pb)
            done += n

    def tm_proj_post(self, actT_fn, act_buf_fn, nK, W, T, xres, xres_bufs, xout, xout_bufs, resid_w):
        P, cx = self.P, self.cx
        t0 = T * TT
        xy, xy_b, ss, ss_b = self.xy, self.xy_b, self.ss, self.ss_b
        for half in range(2):
            for j in range(nK):
                wv, wb = self.load_w(W[j * 128:(j + 1) * 128, half * 1024:(half + 1) * 1024]
                                     .rearrange("p (k n) -> p k n", k=1), 1, 1024, key=(id(W), "tm", half, j))
                for s in range(NSUB):
                    for n in range(2):
                        pbk = s * 2 + n
                        last = (j == nK - 1)
                        cx.op("pe", lambda e: e.matmul(P.ps[pbk][:], lhsT=actT_fn(j, s), rhs=wv[:, 0, n * 512:(n + 1) * 512],
                                                       start=(j == 0), stop=last),
                              reads=[wb, act_buf_fn(j)], writes=[P.psb[pbk]],
                              inc=(last or (s == NSUB - 1 and n == 1)), accum=(j > 0))
            for s in range(NSUB):
                for n in range(2):
                    pbk = s * 2 + n
                    col = half * 2 + n
                    junk, junk_b = self.nextjunk()
                    cx.op("act", lambda e: e.activation(out=junk[:, 0:512], in_=P.ps[pbk][:], func=AF.Square,
                                                        accum_out=ss[:, s * 4 + col:s * 4 + col + 1]),
                          reads=[P.psb[pbk]], writes=[junk_b, ss_b, P.psrd[pbk]], accum=True)
                    d0 = half * 1024 + n * 512
                    cx.op("dve", lambda e: e.tensor_tensor(out=xy[:, s, d0:d0 + 512], in0=P.ps[pbk][:],
                                                           in1=self.gpost[:, d0:d0 + 512], op=ALU.mult),
                          reads=[P.psb[pbk], self.gpost_b, P.psrd[pbk]], writes=[xy_b[s]], accum=True)
        for s in range(NSUB):
            r0 = t0 + s * 128
            xrt, xrb = self.xr[s % 2], self.xr_b[s % 2]
            cx.dma(xrt[:], xres[r0:r0 + 128, :], xrb, reads=[xres_bufs[T * NSUB + s]], writes=[xrb])
            cx.op("dve", lambda e: e.reduce_sum(out=self.rs[:, 4 + s:5 + s], in_=ss[:, s * 4:s * 4 + 4],
                                                axis=mybir.AxisListType.X),
                  reads=[ss_b], writes=[self.rsq_b])
            P.rstd_from_ss(self.rs[:, 4 + s:5 + s], self.rs[:, 4 + s:5 + s], D, [self.rsq_b], [self.rsq_b],
                           scale_extra=resid_w)
            cx.op("dve", lambda e: e.scalar_tensor_tensor(out=xrt[:], in0=xy[:, s, :], scalar=self.rs[:, 4 + s:5 + s],
                                                          in1=xrt[:], op0=ALU.mult, op1=ALU.add),
                  reads=[xy_b[s], self.rsq_b, xrb], writes=[xrb])
            cx.dma(xout[r0:r0 + 128, :], xrt[:], xrb, reads=[xrb], writes=[xout_bufs[T * NSUB + s]])
        for s in range(NSUB):
            xy_b[s].w = list(xy_b[s].w)


class Prog:
    def __init__(self, stages=("ffn1", "mix", "mla", "merge", "xa", "ffn2")):
        self.stages = stages
        self.nc = bass.Bass("TRN2", target_bir_lowering=False)
        self.st = contextlib.ExitStack()
        self.nws = 0

    def dram_in(self, name, shape, dt=F32):
        return self.nc.dram_tensor(name, list(shape), dt, kind="ExternalInput").ap()

    def dram_tmp(self, name, shape, dt):
        return self.nc.dram_tensor(name, list(shape), dt, kind="Internal").ap()

    def build(self):
        nc = self.nc
        with self.st as st:
            cx = self.cx = Cx(nc, st)
            I = self.I = {}
            I["x"] = self.dram_in("x", [S, D])
            I["mem"] = self.dram_in("mem", [N_MEM, D])
            I["positions"] = self.dram_in("positions", [1, S], I32)
            I["hgrn_lb_logits"] = self.dram_in("hgrn_lb_logits", [2, D])
            for n in ("ffn1", "ffn2"):
                I[n + "_pre_g"] = self.dram_in(n + "_pre_g", [1, D])
                I[n + "_w_gate"] = self.dram_in(n + "_w_gate", [D, DFF])
                I[n + "_w_up"] = self.dram_in(n + "_w_up", [D, DFF])
                I[n + "_w_down"] = self.dram_in(n + "_w_down", [DFF, D])
                I[n + "_post_g"] = self.dram_in(n + "_post_g", [1, D])
            for n, shp in (("mix_pre_g", [1, D]), ("w_in", [D, IN_DIM]), ("hg_norm_g", [1, D]),
                           ("mla_q_norm_g", [1, Q_LORA]), ("mla_w_q_up", [Q_LORA, MLA_H * 192]),
                           ("mla_kv_norm_g", [1, KV_LORA]), ("mla_w_kv_up", [KV_LORA, MLA_H * 256]),
                           ("w_branch_a", [D, D]), ("w_branch_b", [D, D]), ("w_out", [D, D]), ("mix_post_g", [1, D]),
                           ("xa_pre_g", [1, D]), ("xa_mem_g", [1, D]), ("xa_w_q", [D, 512]), ("xa_w_k", [D, 512]),
                           ("xa_w_v", [D, 512]), ("xa_w_o", [512, D]), ("xa_post_g", [1, D])):
                I[n] = self.dram_in(n, shp)
            I["ident"] = self.dram_in("ident_in", [128, 128])
            I["tri"] = self.dram_in("tri_in", [128, 512])
            I["cmask"] = self.dram_in("cmask_in", [128, 128])
            I["invf"] = self.dram_in("invf_in", [64, 1])
            self.out = nc.dram_tensor("out", [S, D], F32, kind="ExternalOutput").ap()
            nb = S // 128
            self.out_bufs = [cx.buf(f"out{i}") for i in range(nb)]
            self.x_bufs = [cx.buf(f"xin{i}") for i in range(nb)]
            X = {}
            for n in ("X1", "X2", "X3"):
                X[n] = (self.dram_tmp(n, [S, D], F32), [cx.buf(f"{n}_{i}") for i in range(nb)])
            self.YA = self.dram_tmp("YA", [D, S], BF); self.YA_b = [cx.buf(f"YA{i}") for i in range(NT)]
            self.GB = self.dram_tmp("GB", [D, S], BF); self.GB_b = [cx.buf(f"GB{i}") for i in range(NT)]
            self.OB = self.dram_tmp("OB", [D, S], BF); self.OB_b = [cx.buf(f"OB{i}") for i in range(NT)]
            self.CQ = self.dram_tmp("CQ", [Q_LORA, S], BF); self.CQ_b = [cx.buf(f"CQ{i}") for i in range(NT)]
            self.CKV = self.dram_tmp("CKV", [KV_LORA, S], BF); self.CKV_b = [cx.buf(f"CKV{i}") for i in range(NT)]
            self.KR = self.dram_tmp("KR", [64, S], BF); self.KR_b = [cx.buf(f"KR{i}") for i in range(NT)]
            self.CS = self.dram_tmp("CS", [2, 64, S], F32); self.CS_b = cx.buf("CS")

            self.ps = [st.enter_context(nc.psum_tensor(f"ps{i}", [128, 512], F32)) for i in range(8)]
            self.psb = [cx.buf(f"psb{i}") for i in range(8)]
            self.psrd = [cx.buf(f"psrd{i}") for i in range(8)]
            self.ident = cx.sb("ident", [128, 128], BF); self.ident_b = cx.buf("ident")
            self.tri = cx.sb("tri", [128, 512], F32); self.tri_b = cx.buf("tri")
            self.cmask = cx.sb("cmask", [128, 128], BF); self.cmask_b = cx.buf("cmask")
            self.ones = cx.sb("ones", [128, 128], BF); self.ones_b = cx.buf("ones")
            tmpf = cx.sb("tmpf", [128, 128], F32); tmpf_b = cx.buf("tmpf")
            cx.dma(tmpf[:], I["ident"], tmpf_b, writes=[tmpf_b])
            cx.op("dve", lambda e: e.tensor_copy(out=self.ident[:], in_=tmpf[:]), reads=[tmpf_b], writes=[self.ident_b])
            cx.dma(tmpf[:], I["cmask"], tmpf_b, writes=[tmpf_b])
            cx.op("dve", lambda e: e.tensor_copy(out=self.cmask[:], in_=tmpf[:]), reads=[tmpf_b], writes=[self.cmask_b])
            cx.dma(self.tri[:], I["tri"], self.tri_b, writes=[self.tri_b])
            cx.op("pool", lambda e: e.memset(self.ones[:], 1.0), writes=[self.ones_b])
            self.eps_ap(1.0); self.eps_ap(0.5)
            cx.barrier()

            chain = [("ffn1", I["x"], self.x_bufs)]
            cur, cur_b = I["x"], self.x_bufs
            order = [s_ for s_ in ("ffn1", "mix", "xa", "ffn2") if (s_ in self.stages) or (s_ == "mix" and "merge" in self.stages)]
            nxt = {"ffn1": "X1", "mix": "X2", "xa": "X3"}
            for i, sname in enumerate(order):
                lastst = (i == len(order) - 1)
                dst, dst_b = (self.out, self.out_bufs) if lastst else X[nxt[sname]]
                if sname in ("ffn1", "ffn2"):
                    self.ffn_phase(sname, cur, cur_b, dst, dst_b)
                elif sname == "mix":
                    self.rope_tables()
                    self.mix_phase(cur, cur_b)
                    self.mla_phase()
                    self.merge_phase(cur, cur_b, dst, dst_b)
                elif sname == "xa":
                    self.xa_phase(cur, cur_b, dst, dst_b)
                cur, cur_b = dst, dst_b

            sp = cx.E["sp"]
            for b in self.out_bufs:
                for d in b.w:
                    cx._wait(sp, d)
        return nc

    def rstd_from_ss(self, ss_ap, out_ap, n, reads, writes, scale_extra=1.0):
        cx = self.cx
        cx.op("act", lambda e: e.activation(out=out_ap, in_=ss_ap, func=AF.Sqrt,
                                             scale=1.0 / (n * scale_extra * scale_extra),
                                             bias=self.eps_ap(scale_extra)),
              reads=reads, writes=writes)
        cx.op("dve", lambda e: e.reciprocal(out=out_ap, in_=out_ap), reads=writes, writes=writes)

    def rstd_bc(self, rstd, rstd_b, pn, n):
        cx = self.cx
        if OPT_ACT_RSTD:
            cx.op("act", lambda e: e.activation(out=rstd[:], in_=self.ps[pn][:], func=AF.Ln, scale=1.0 / n,
                                                bias=self.eps_ap(1.0)), reads=[self.psb[pn]], writes=[rstd_b])
            cx.op("act", lambda e: e.activation(out=rstd[:], in_=rstd[:], func=AF.Exp, scale=-0.5),
                  reads=[rstd_b], writes=[rstd_b])
        else:
            cx.op("act", lambda e: e.activation(out=rstd[:], in_=self.ps[pn][:], func=AF.Sqrt, scale=1.0 / n,
                                                bias=self.eps_ap(1.0)), reads=[self.psb[pn]], writes=[rstd_b])
            cx.op("dve", lambda e: e.reciprocal(out=rstd[:], in_=rstd[:]), reads=[rstd_b], writes=[rstd_b])

    def recip_bc(self, rec, rec_b, pd):
        cx = self.cx
        if OPT_ACT_RSTD:
            cx.op("act", lambda e: e.activation(out=rec[:], in_=self.ps[pd][:], func=AF.Ln), reads=[self.psb[pd]],
                  writes=[rec_b])
            cx.op("act", lambda e: e.activation(out=rec[:], in_=rec[:], func=AF.Exp, scale=-1.0), reads=[rec_b],
                  writes=[rec_b])
        else:
            cx.op("dve", lambda e: e.reciprocal(out=rec[:], in_=self.ps[pd][:]), reads=[self.psb[pd]], writes=[rec_b])

    def eps_ap(self, scale_extra):
        key = float(scale_extra)
        if not hasattr(self, "_eps"):
            self._eps = {}
        if key not in self._eps:
            t = self.st.enter_context(self.nc.sbuf_tensor(f"eps{len(self._eps)}", [128, 1], F32))
            b = self.cx.buf("eps")
            self.cx.op("pool", lambda e: e.memset(t[:], EPS / (key * key)), writes=[b])
            self._eps[key] = (t, b)
        return self._eps[key][0][:]

    def transpose_to(self, src, src_b, nchunks, dst_fn, dst_b, np_in=128):
        cx = self.cx
        done = 0
        first = True
        while done < nchunks:
            n = min(8, nchunks - done)
            pb = self._tbank = (getattr(self, "_tbank", -1) + 1) % 2
            pv = self.ps[pb][:].bitcast(BF)
            for c in range(n):
                cc = done + c
                cx.op("pe", lambda e: e.transpose(out=pv[:, c * 128:c * 128 + np_in],
                                                  in_=src[0:np_in, cc * 128:(cc + 1) * 128],
                                                  identity=self.ident[0:np_in, 0:np_in]),
                      reads=[src_b, self.ident_b], writes=[self.psb[pb]], inc=(c == n - 1), accum=(c > 0))
            srcv = pv[:, 0:n * 128].rearrange("p (c t) -> p c t", c=n)[:, :, 0:np_in]
            eng = "dve" if (done // 8) % 2 == 0 else "act"
            dst = dst_fn(done, n)
            if eng == "act":
                cx.op("act", lambda e: e.copy(out=dst, in_=srcv), reads=[self.psb[pb]], writes=[dst_b], accum=not first)
            else:
                cx.op("dve", lambda e: e.tensor_copy(out=dst, in_=srcv), reads=[self.psb[pb]], writes=[dst_b],
                      accum=not first)
            first = False
            done += n

    def ffn_phase(self, name, xin, xin_bufs, xout, xout_bufs):
        cx, I = self.cx, self.I
        wg, wu, wd = I[name + "_w_gate"], I[name + "_w_up"], I[name + "_w_down"]
        with contextlib.ExitStack() as st:
            old_st = cx.st; mark = len(cx.dma_sems)
            ph = PH(self, st, I[name + "_pre_g"], I[name + "_post_g"])
            NJ = DFF // 128
            actT = cx.sb("actT", [128, NJ, TT], BF); actT_b = [cx.buf(f"actT{j}") for j in range(NJ)]
            sg = [cx.sb(f"sg{i}", [128, TT], F32) for i in range(2)]; sg_b = [cx.buf(f"sg{i}") for i in range(2)]
            for T in range(NT):
                ph.prenorm(xin, xin_bufs, T)
                for g in range(DFF // 256):
                    wgv, wgb = ph.load_wcols(wg, g * 256, 256, key=("g", g))
                    wuv, wub = ph.load_wcols(wu, g * 256, 256, key=("u", g))
                    for jj in range(2):
                        j = g * 2 + jj
                        pg, pu = ph.bank(), ph.bank()
                        for (wv, wb, pbk) in ((wgv, wgb, pg), (wuv, wub, pu)):
                            for k in range(KC):
                                cx.op("pe", lambda e: e.matmul(self.ps[pbk][:], lhsT=wv[:, k, jj * 128:(jj + 1) * 128],
                                                               rhs=ph.hT[:, k, :], start=(k == 0), stop=(k == KC - 1)),
                                      reads=[wb] + ph.hT_b, writes=[self.psb[pbk]], inc=(k == KC - 1), accum=(k > 0))
                        sgt, sgb = sg[j % 2], sg_b[j % 2]
                        cx.op("act", lambda e: e.activation(out=sgt[:], in_=self.ps[pg][:], func=AF.Silu),
                              reads=[self.psb[pg]], writes=[sgb])
                        cx.op("dve", lambda e: e.tensor_tensor(out=actT[:, j, :], in0=self.ps[pu][:], in1=sgt[:],
                                                               op=ALU.mult),
                              reads=[self.psb[pu], sgb], writes=[actT_b[j]])
                ph.tm_proj_post(lambda j, s: actT[:, j, s * 128:(s + 1) * 128], lambda j: actT_b[j], NJ, wd, T,
                                xin, xin_bufs, xout, xout_bufs, 0.5)
            cx.end_phase(mark)
            cx.st = old_st

    def vecT(self, name, src, ncol):
        cx = self.cx
        t = cx.sb(name, [128, ncol], F32); b = cx.buf(name)
        v = src.rearrange("o (c p) -> p (o c)", p=128)
        for c in range(ncol):
            cx.dma(t[:, c:c + 1], v[:, c:c + 1], b, writes=[b], partial=(c > 0), allow_slow_non_contiguous=True)
        return t, b

    def rope_tables(self):
        cx, I = self.cx, self.I
        PI = 3.141592653589793
        with contextlib.ExitStack() as st:
            old_st, cx.st = cx.st, st; mark = len(cx.dma_sems)
            posi = cx.sb("posi", [64, S], I32); b0 = cx.buf("posi")
            cx.dma(posi[:], I["positions"].partition_broadcast(64), b0, writes=[b0])
            invf = cx.sb("invf", [64, 1], F32); b1 = cx.buf("invf")
            cx.dma(invf[:], I["invf"], b1, writes=[b1])
            ang = cx.sb("ang", [64, S], F32); ba = cx.buf("ang")
            cx.op("dve", lambda e: e.tensor_copy(out=ang[:], in_=posi[:]), reads=[b0], writes=[ba])
            cx.op("dve", lambda e: e.tensor_scalar(out=ang[:], in0=ang[:], scalar1=invf[:, 0:1], scalar2=None,
                                                   op0=ALU.mult), reads=[ba, b1], writes=[ba])
            a2 = cx.sb("a2", [64, S], F32); b2 = cx.buf("a2")
            ki = cx.sb("ki", [64, S], I32); bk = cx.buf("ki")
            kf = cx.sb("kf", [64, S], F32); bf_ = cx.buf("kf")
            r = cx.sb("r", [64, S], F32); br = cx.buf("r")
            m = cx.sb("m", [64, S], F32); bm = cx.buf("m")
            for which, shift in ((0, PI / 2), (1, 0.0)):
                cx.op("dve", lambda e: e.tensor_scalar(out=a2[:], in0=ang[:], scalar1=shift, scalar2=None, op0=ALU.add),
                      reads=[ba], writes=[b2])
                cx.op("dve", lambda e: e.tensor_scalar(out=kf[:], in0=a2[:], scalar1=1.0 / TWO_PI, scalar2=None,
                                                       op0=ALU.mult), reads=[b2], writes=[bf_])
                cx.op("dve", lambda e: e.tensor_copy(out=ki[:], in_=kf[:]), reads=[bf_], writes=[bk])
                cx.op("dve", lambda e: e.tensor_copy(out=kf[:], in_=ki[:]), reads=[bk], writes=[bf_])
                cx.op("dve", lambda e: e.scalar_tensor_tensor(out=r[:], in0=kf[:], scalar=-C1, in1=a2[:],
                                                              op0=ALU.mult, op1=ALU.add), reads=[bf_, b2], writes=[br])
                cx.op("dve", lambda e: e.scalar_tensor_tensor(out=r[:], in0=kf[:], scalar=-C2, in1=r[:],
                                                              op0=ALU.mult, op1=ALU.add), reads=[bf_, br], writes=[br])
                cx.op("dve", lambda e: e.tensor_scalar(out=m[:], in0=r[:], scalar1=PI, scalar2=TWO_PI, op0=ALU.is_gt,
                                                       op1=ALU.mult), reads=[br], writes=[bm])
                cx.op("dve", lambda e: e.tensor_tensor(out=r[:], in0=r[:], in1=m[:], op=ALU.subtract),
                      reads=[br, bm], writes=[br])
                cx.op("dve", lambda e: e.tensor_scalar(out=m[:], in0=r[:], scalar1=-PI, scalar2=TWO_PI, op0=ALU.is_lt,
                                                       op1=ALU.mult), reads=[br], writes=[bm])
                cx.op("dve", lambda e: e.tensor_tensor(out=r[:], in0=r[:], in1=m[:], op=ALU.add),
                      reads=[br, bm], writes=[br])
                cx.op("act", lambda e: e.activation(out=m[:], in_=r[:], func=AF.Sin), reads=[br], writes=[bm])
                cx.dma(self.CS[which], m[:], bm, reads=[bm], writes=[self.CS_b], partial=True)
            cx.end_phase(mark)
            cx.st = old_st

    def mix_phase(self, xin, xin_bufs):
        cx, I = self.cx, self.I
        W = I["w_in"]
        OFF_Q, OFF_F, OFF_I, OFF_OG, OFF_CQ, OFF_CKV, OFF_KPE, OFF_GA, OFF_GB = 0, 2048, 4096, 6144, 8192, 8704, 9216, 9280, 11328
        with contextlib.ExitStack() as st:
            old_st = cx.st; mark = len(cx.dma_sems)
            ph = PH(self, st, I["mix_pre_g"], None, xy=False, nwb=2)
            lg, lg_b = self.vecT("lg", I["hgrn_lb_logits"][0:1, :], 16)
            lg1, lg1_b = self.vecT("lg1", I["hgrn_lb_logits"][1:2, :], 16)
            hgT, hgT_b = self.vecT("hgT", I["hg_norm_g"], 16)
            qgT, qgT_b = self.vecT("qgT", I["mla_q_norm_g"], 4)
            kgT, kgT_b = self.vecT("kgT", I["mla_kv_norm_g"], 4)
            lb = cx.sb("lb", [128, 16], F32); oml = cx.sb("oml", [128, 16], F32); noml = cx.sb("noml", [128, 16], F32)
            lb_b = cx.buf("lb")
            cx.op("dve", lambda e: e.tensor_tensor(out=lb[:], in0=lg[:], in1=lg1[:], op=ALU.subtract),
                  reads=[lg_b, lg1_b], writes=[lb_b])
            cx.op("act", lambda e: e.activation(out=lb[:], in_=lb[:], func=AF.Sigmoid), reads=[lb_b], writes=[lb_b])
            cx.op("dve", lambda e: e.tensor_scalar(out=oml[:], in0=lb[:], scalar1=-1.0, scalar2=1.0, op0=ALU.mult,
                                                   op1=ALU.add), reads=[lb_b], writes=[lb_b], accum=True)
            cx.op("dve", lambda e: e.tensor_scalar(out=noml[:], in0=oml[:], scalar1=-1.0, scalar2=None, op0=ALU.mult),
                  reads=[lb_b], writes=[lb_b], accum=True)
            rmask = cx.sb("rmask", [128, TT], F32); rmask_b = cx.buf("rmask")
            cx.op("pool", lambda e: e.memset(rmask[:], 1.0), writes=[rmask_b])
            cx.op("pool", lambda e: e.memset(rmask[:].rearrange("p (a b) -> p a b", b=64)[:, :, 0:1], 0.0),
                  writes=[rmask_b], accum=True)
            GH = 4
            f32t = {}
            for n in ("sgm", "lf", "bb", "bp", "eb", "enb"):
                f32t[n] = (cx.sb("t_" + n, [128, TT], F32), cx.buf("t_" + n))
            qtT = cx.sb("qtT", [128, GH, TT], BF); qtT_b = [cx.buf(f"qtT{i}") for i in range(GH)]
            ktT = cx.sb("ktT", [128, GH, TT], BF); ktT_b = [cx.buf(f"ktT{i}") for i in range(GH)]
            vT = cx.sb("vT", [128, TT], BF); vT_b = cx.buf("vT")
            v_tm = cx.sb("v_tm", [128, GH, NSUB, 128], BF); v_tm_b = [cx.buf(f"v_tm{i}") for i in range(GH)]
            k_tm = cx.sb("k_tm", [128, GH, NSUB, 128], BF); k_tm_b = [cx.buf(f"k_tm{i}") for i in range(GH)]
            sog = cx.sb("sog", [128, GH, TT], BF); sog_b = [cx.buf(f"sog{i}") for i in range(GH)]
            osb = cx.sb("osb", [128, GH, TT], F32); osb_b = [cx.buf(f"osb{i}") for i in range(GH)]
            lat, lat_b = osb, osb_b
            sqo = cx.sb("sqo", [128, GH, TT], BF); sqo_b = [cx.buf(f"sqo{i}") for i in range(GH)]
            rstd = cx.sb("rstd", [128, TT], F32); rstd_b = cx.buf("rstd")
            AT4 = cx.sb("AT4", [128, GH, 128], BF); AT4_b = cx.buf("AT4")
            khT = cx.sb("khT", [128, TT], BF); khT_b = cx.buf("khT")
            state = cx.sb("state", [128, HG_H, 128], F32); state_b = [cx.buf(f"state{i}") for i in range(HG_H)]
            stbf = cx.sb("stbf", [128, GH, 2, 128], BF); stbf_b = [cx.buf(f"stbf{i}") for i in range(GH)]
            kvt = cx.sb("kvt", [128, GH, 128], F32); kvt_b = [cx.buf(f"kvt{i}") for i in range(GH)]
            ev = cx.sb("ev", [128, GH, 3, 2 * NSUB], F32); ev_b = [cx.buf(f"ev{i}") for i in range(GH)]
            oaT = cx.sb("oaT", [128, HG_H, TT], BF); oaT_b = [cx.buf(f"oaT{i}") for i in range(HG_H)]
            ga = cx.sb("ga", [128, TT], F32); ga_b = cx.buf("ga")
            yag = [cx.sb(f"yag{i}", [128, TT], BF) for i in range(2)]; yag_b = [cx.buf(f"yag{i}") for i in range(2)]
            latn = cx.sb("latn", [128, 4, TT], BF); latn_b = cx.buf("latn")
            cst = cx.sb("cst", [64, 2, TT], F32); cst_b = cx.buf("cst")
            kp = cx.sb("kp", [64, 2, TT], F32); kp_b = cx.buf("kp")
            krt = cx.sb("krt", [64, TT], BF); krt_b = cx.buf("krt")
            wrot = cx.sb("wrot", [128, KC, 64], BF); wrot_b = cx.buf("wrot")
            for h in range(HG_H):
                cx.op("pool", lambda e: e.memset(state[:, h, :], 0.0), writes=[state_b[h]])
            hT, hT_b = ph.hT, ph.hT_b
            rhs = lambda k: hT[:, k, :]

            for T in range(NT):
                t0 = T * TT
                ph.prenorm(xin, xin_bufs, T)
                for g in range(HG_H // GH):
                    def f_cb(ci, pb):
                        hl = ci; h = g * GH + hl
                        sgm, sgm_b = f32t["sgm"]; lf, lf_b = f32t["lf"]; bb, bb_b = f32t["bb"]; bp, bp_b = f32t["bp"]
                        eb, eb_b = f32t["eb"]; enb, enb_b = f32t["enb"]
                        cx.op("act", lambda e: e.activation(out=sgm[:], in_=self.ps[pb][:], func=AF.Sigmoid),
                              reads=[self.psb[pb]], writes=[sgm_b])
                        cx.op("act", lambda e: e.activation(out=lf[:], in_=sgm[:], func=AF.Ln, scale=oml[:, h:h + 1],
                                                            bias=lb[:, h:h + 1]), reads=[sgm_b, lb_b], writes=[lf_b])
                        cx.op("dve", lambda e: e.tensor_tensor_scan(out=bb[:], data0=rmask[:], data1=lf[:], initial=0.0,
                                                                    op0=ALU.mult, op1=ALU.add),
                              reads=[rmask_b, lf_b], writes=[bb_b])
                        b3 = bb[:].rearrange("p (a b) -> p a b", b=64)
                        cx.op("dve", lambda e: e.tensor_tensor(out=bp[:].rearrange("p (a b) -> p a b", b=64), in0=b3,
                                                               in1=b3[:, :, 31:32].to_broadcast([128, 2 * NSUB, 64]),
                                                               op=ALU.subtract), reads=[bb_b], writes=[bp_b])
                        cx.op("dve", lambda e: e.tensor_scalar(out=bp[:], in0=bp[:], scalar1=40.0, scalar2=-40.0,
                                                               op0=ALU.min, op1=ALU.max), reads=[bp_b], writes=[bp_b])
                        cx.op("act", lambda e: e.activation(out=eb[:], in_=bp[:], func=AF.Exp), reads=[bp_b], writes=[eb_b])
                        cx.op("act", lambda e: e.activation(out=enb[:], in_=bp[:], func=AF.Exp, scale=-1.0),
                              reads=[bp_b], writes=[enb_b])
                        cx.op("act", lambda e: e.activation(out=ev[:, hl, 0, :], in_=b3[:, :, 63], func=AF.Exp),
                              reads=[bb_b], writes=[ev_b[hl]])
                        cx.op("act", lambda e: e.activation(out=ev[:, hl, 1, :], in_=b3[:, :, 31], func=AF.Exp),
                              reads=[bb_b], writes=[ev_b[hl]], accum=True)
                        cx.op("act", lambda e: e.activation(out=ev[:, hl, 2, :],
                                                            in_=bp[:].rearrange("p (a b) -> p a b", b=64)[:, :, 63],
                                                            func=AF.Exp), reads=[bp_b], writes=[ev_b[hl]], accum=True)
                        cx.op("dve", lambda e: e.tensor_scalar(out=sgm[:], in0=sgm[:], scalar1=noml[:, h:h + 1],
                                                               scalar2=oml[:, h:h + 1], op0=ALU.mult, op1=ALU.add),
                              reads=[sgm_b, lb_b], writes=[sgm_b])
                        cx.op("dve", lambda e: e.tensor_tensor(out=ktT[:, hl, :], in0=sgm[:], in1=enb[:], op=ALU.mult),
                              reads=[sgm_b, enb_b], writes=[ktT_b[hl]])
                        cx.op("pool", lambda e: e.tensor_copy(out=qtT[:, hl, :], in_=eb[:]), reads=[eb_b],
                              writes=[qtT_b[hl]])
                        enb3 = enb[:].rearrange("p (a b) -> p a b", b=64)
                        cx.op("dve", lambda e: e.tensor_tensor(out=enb3, in0=enb3,
                                                               in1=ev[:, hl, 2, :].unsqueeze(2).to_broadcast([128, 2 * NSUB, 64]),
                                                               op=ALU.mult), reads=[enb_b, ev_b[hl]], writes=[enb_b])
                        cx.op("dve", lambda e: e.tensor_tensor(out=khT[:], in0=sgm[:], in1=enb[:], op=ALU.mult),
                              reads=[sgm_b, enb_b], writes=[khT_b])
                        self.transpose_to(khT, khT_b, NSUB, lambda c0, n: k_tm[:, hl, c0:c0 + n, :], k_tm_b[hl])
                    ph.proj_fm(W, OFF_F + g * GH * 128, GH * 128, rhs, hT_b, f_cb)

                    def q_cb(ci, pb):
                        hl = ci
                        sq, sq_b = f32t["lf"]
                        cx.op("act", lambda e: e.activation(out=sq[:], in_=self.ps[pb][:], func=AF.Silu),
                              reads=[self.psb[pb]], writes=[sq_b])
                        cx.op("dve", lambda e: e.tensor_tensor(out=qtT[:, hl, :], in0=sq[:], in1=qtT[:, hl, :], op=ALU.mult),
                              reads=[sq_b, qtT_b[hl]], writes=[qtT_b[hl]])
                    ph.proj_fm(W, OFF_Q + g * GH * 128, GH * 128, rhs, hT_b, q_cb)

                    def i_cb(ci, pb):
                        hl = ci
                        cx.op("act", lambda e: e.copy(out=vT[:], in_=self.ps[pb][:]), reads=[self.psb[pb]], writes=[vT_b])
                        self.transpose_to(vT, vT_b, NSUB, lambda c0, n: v_tm[:, hl, c0:c0 + n, :], v_tm_b[hl])
                    ph.proj_fm(W, OFF_I + g * GH * 128, GH * 128, rhs, hT_b, i_cb)

                    def og_cb(ci, pb):
                        hl = ci
                        cx.op("act", lambda e: e.activation(out=sog[:, hl, :], in_=self.ps[pb][:], func=AF.Silu),
                              reads=[self.psb[pb]], writes=[sog_b[hl]])
                    ph.proj_fm(W, OFF_OG + g * GH * 128, GH * 128, rhs, hT_b, og_cb)

                    def stage_pe1(i):
                        tsl = slice(i * 128, (i + 1) * 128)
                        pa = ph.bank()
                        for hl in range(GH):
                            cx.op("pe", lambda e: e.matmul(self.ps[pa][:, hl * 128:(hl + 1) * 128], lhsT=ktT[:, hl, tsl],
                                                           rhs=qtT[:, hl, tsl], start=True, stop=True, skip_group_check=True),
                                  reads=[ktT_b[hl], qtT_b[hl]], writes=[self.psb[pa]], accum=(hl > 0))
                        pk = [ph.bank(), ph.bank()]
                        for hl in range(GH):
                            for j in range(2):
                                psl = slice(j * 64, (j + 1) * 64)
                                cx.op("pe", lambda e: e.matmul(self.ps[pk[j]][:, hl * 128:(hl + 1) * 128],
                                                               lhsT=k_tm[psl, hl, i, :], rhs=v_tm[psl, hl, i, :],
                                                               start=True, stop=True, skip_group_check=True),
                                      reads=[k_tm_b[hl], v_tm_b[hl]], writes=[self.psb[pk[j]]], accum=(hl > 0))
                        return pa, pk
                    nxt = stage_pe1(0)
                    for i in range(NSUB):
                        tsl = slice(i * 128, (i + 1) * 128)
                        pa, pk = nxt
                        cx.op("dve", lambda e: e.tensor_tensor(out=AT4[:].rearrange("p h t -> p (h t)"), in0=self.ps[pa][:],
                                                               in1=self.tri[:], op=ALU.mult),
                              reads=[self.psb[pa], self.tri_b], writes=[AT4_b])
                        for hl in range(GH):
                            h = g * GH + hl
                            for j in range(2):
                                c = 2 * i + j
                                q_ = hl * 2 + j
                                cx.op("dve", lambda e: e.tensor_scalar(out=stbf[:, hl, j, :], in0=state[:, h, :],
                                                                       scalar1=ev[:, hl, 1, c:c + 1], scalar2=None,
                                                                       op0=ALU.mult),
                                      reads=[state_b[h], ev_b[hl]], writes=[stbf_b[hl]], accum=(j > 0))
                                if OPT_PSUM_STT:
                                    cx.op("dve", lambda e: e.scalar_tensor_tensor(
                                        out=state[:, h, :], in0=state[:, h, :], scalar=ev[:, hl, 0, c:c + 1],
                                        in1=self.ps[pk[j]][:, hl * 128:(hl + 1) * 128],
                                        op0=ALU.mult, op1=ALU.add),
                                          reads=[state_b[h], ev_b[hl], self.psb[pk[j]]], writes=[state_b[h]])
                                else:
                                    cx.op("dve", lambda e: e.tensor_copy(
                                        out=kvt[:, hl, :], in_=self.ps[pk[j]][:, hl * 128:(hl + 1) * 128]),
                                          reads=[self.psb[pk[j]]], writes=[kvt_b[hl]])
                                    cx.op("dve", lambda e: e.scalar_tensor_tensor(
                                        out=state[:, h, :], in0=state[:, h, :], scalar=ev[:, hl, 0, c:c + 1],
                                        in1=kvt[:, hl, :], op0=ALU.mult, op1=ALU.add),
                                          reads=[state_b[h], ev_b[hl], kvt_b[hl]], writes=[state_b[h]])
                        if i + 1 < NSUB:
                            nxt = stage_pe1(i + 1)
                        po = ph.bank()
                        for hl in range(GH):
                            o0 = hl * 128
                            cx.op("pe", lambda e: e.matmul(self.ps[po][:, o0:o0 + 128], lhsT=v_tm[:, hl, i, :], rhs=AT4[:, hl, :],
                                                           start=True, stop=False, skip_group_check=True),
                                  reads=[v_tm_b[hl], AT4_b], writes=[self.psb[po]], inc=False, accum=(hl > 0))
                            for j in range(2):
                                csl = slice(i * 128 + j * 64, i * 128 + (j + 1) * 64)
                                cx.op("pe", lambda e: e.matmul(self.ps[po][:, o0 + j * 64:o0 + (j + 1) * 64],
                                                               lhsT=stbf[:, hl, j, :], rhs=qtT[:, hl, csl],
                                                               start=False, stop=(j == 1), skip_group_check=True),
                                      reads=[stbf_b[hl], qtT_b[hl]], writes=[self.psb[po]], inc=(j == 1), accum=True)
                        cx.op("act", lambda e: e.copy(out=osb[:, :, tsl],
                                                      in_=self.ps[po][:].rearrange("p (h t) -> p h t", h=GH)),
                              reads=[self.psb[po]], writes=osb_b, accum=(i > 0))
                    for hl in range(GH):
                        h = g * GH + hl
                        cx.op("act", lambda e: e.activation(out=sqo[:, hl, :], in_=osb[:, hl, :], func=AF.Square),
                              reads=[osb_b[hl]], writes=[sqo_b[hl]])
                        pn = ph.bank()
                        cx.op("pe", lambda e: e.matmul(self.ps[pn][:], lhsT=self.ones[:], rhs=sqo[:, hl, :],
                                                       start=True, stop=True),
                              reads=[self.ones_b, sqo_b[hl]], writes=[self.psb[pn]])
                        self.rstd_bc(rstd, rstd_b, pn, 128)
                        cx.op("dve", lambda e: e.scalar_tensor_tensor(out=osb[:, hl, :], in0=osb[:, hl, :],
                                                                      scalar=hgT[:, h:h + 1], in1=rstd[:],
                                                                      op0=ALU.mult, op1=ALU.mult),
                              reads=[osb_b[hl], hgT_b, rstd_b], writes=[osb_b[hl]])
                        cx.op("dve", lambda e: e.tensor_tensor(out=oaT[:, h, :], in0=osb[:, hl, :], in1=sog[:, hl, :],
                                                               op=ALU.mult),
                              reads=[osb_b[hl], sog_b[hl]], writes=[oaT_b[h]])
                for c in range(KC):
                    def ga_cb(ci, pb):
                        cx.op("act", lambda e: e.activation(out=ga[:], in_=self.ps[pb][:], func=AF.Sigmoid),
                              reads=[self.psb[pb]], writes=[ga_b])
                    ph.proj_fm(W, OFF_GA + c * 128, 128, rhs, hT_b, ga_cb)

                    def ya_cb(ci, pb):
                        yt, ytb = yag[c % 2], yag_b[c % 2]
                        cx.op("dve", lambda e: e.tensor_tensor(out=yt[:], in0=self.ps[pb][:], in1=ga[:], op=ALU.mult),
                              reads=[self.psb[pb], ga_b], writes=[ytb])
                        cx.dma(self.YA[c * 128:(c + 1) * 128, t0:t0 + TT], yt[:], ytb, reads=[ytb], writes=[self.YA_b[T]],
                               partial=True)
                    ph.proj_fm(I["w_branch_a"], c * 128, 128, lambda k: oaT[:, k, :], oaT_b, ya_cb)
                def gb_cb(ci, pb):
                    yt, ytb = yag[ci % 2], yag_b[ci % 2]
                    cx.op("act", lambda e: e.activation(out=yt[:], in_=self.ps[pb][:], func=AF.Sigmoid),
                          reads=[self.psb[pb]], writes=[ytb])
                    cx.dma(self.GB[ci * 128:(ci + 1) * 128, t0:t0 + TT], yt[:], ytb, reads=[ytb], writes=[self.GB_b[T]],
                           partial=True)
                ph.proj_fm(W, OFF_GB, D, rhs, hT_b, gb_cb)
                for (off, gT, gTb, dst, dst_b) in ((OFF_CQ, qgT, qgT_b, self.CQ, self.CQ_b),
                                                  (OFF_CKV, kgT, kgT_b, self.CKV, self.CKV_b)):
                    def lat_cb(ci, pb):
                        cx.op("act", lambda e: e.copy(out=lat[:, ci, :], in_=self.ps[pb][:]), reads=[self.psb[pb]],
                              writes=[lat_b[ci]])
                        cx.op("act", lambda e: e.activation(out=sqo[:, ci, :], in_=lat[:, ci, :], func=AF.Square),
                              reads=[lat_b[ci]], writes=[sqo_b[ci]])
                    ph.proj_fm(W, off, 512, rhs, hT_b, lat_cb)
                    pn = ph.bank()
                    for ci in range(4):
                        cx.op("pe", lambda e: e.matmul(self.ps[pn][:], lhsT=self.ones[:], rhs=sqo[:, ci, :],
                                                       start=(ci == 0), stop=(ci == 3)),
                              reads=[self.ones_b, sqo_b[ci]], writes=[self.psb[pn]], inc=(ci == 3), accum=(ci > 0))
                    self.rstd_bc(rstd, rstd_b, pn, 512)
                    for ci in range(4):
                        cx.op("dve", lambda e: e.scalar_tensor_tensor(out=latn[:, ci, :], in0=lat[:, ci, :],
                                                                      scalar=gT[:, ci:ci + 1], in1=rstd[:],
                                                                      op0=ALU.mult, op1=ALU.mult),
                              reads=[lat_b[ci], gTb, rstd_b], writes=[latn_b], accum=(ci > 0))
                    cx.dma(dst[:, t0:t0 + TT].rearrange("(c p) t -> p c t", p=128), latn[:], latn_b, reads=[latn_b],
                           writes=[dst_b[T]])
                cx.dma(cst[:], self.CS[:, :, t0:t0 + TT].rearrange("w p t -> p w t"), cst_b, reads=[self.CS_b],
                       writes=[cst_b])
                wv, wb = ph.load_wcols(W, OFF_KPE, 64)
                cx.op("act", lambda e: e.mul(out=wrot[:, :, 0:32], in_=wv[:, :, 32:64], mul=-1.0), reads=[wb],
                      writes=[wrot_b])
                cx.op("act", lambda e: e.copy(out=wrot[:, :, 32:64], in_=wv[:, :, 0:32]), reads=[wb], writes=[wrot_b],
                      accum=True)
                for wi, (wt, wtb) in enumerate(((wv, wb), (wrot, wrot_b))):
                    pb = ph.bank()
                    for k in range(KC):
                        cx.op("pe", lambda e: e.matmul(self.ps[pb][0:64, :], lhsT=wt[:, k, 0:64], rhs=hT[:, k, :],
                                                       start=(k == 0), stop=(k == KC - 1)),
                              reads=[wtb] + hT_b, writes=[self.psb[pb]], inc=(k == KC - 1), accum=(k > 0))
                    cx.op("dve", lambda e: e.tensor_tensor(out=kp[:, wi, :], in0=self.ps[pb][0:64, :], in1=cst[:, wi, :],
                                                           op=ALU.mult),
                          reads=[self.psb[pb], cst_b], writes=[kp_b], accum=(wi > 0))
                cx.op("dve", lambda e: e.tensor_tensor(out=krt[:], in0=kp[:, 0, :], in1=kp[:, 1, :], op=ALU.add),
                      reads=[kp_b], writes=[krt_b])
                cx.dma(self.KR[:, t0:t0 + TT], krt[:], krt_b, reads=[krt_b], writes=[self.KR_b[T]])
            cx.end_phase(mark)
            cx.st = old_st

    def mla_phase(self):
        cx, I = self.cx, self.I
        SC = 192 ** -0.5
        with contextlib.ExitStack() as st:
            old_st = cx.st; mark = len(cx.dma_sems)
            ph = PH(self, st, None, None, xy=False, wcols=64)
            cqT = cx.sb("cqT", [128, 4, S], BF); cq_b = cx.buf("cqT")
            ckT = cx.sb("ckT", [128, 4, S], BF); ck_b = cx.buf("ckT")
            krT = cx.sb("krT", [64, S], BF); kr_b = cx.buf("krT")
            cs = cx.sb("cs", [64, 2, S], F32); cs_b = cx.buf("cs")
            cx.dma(cqT[:], self.CQ.rearrange("(c p) t -> p c t", p=128), cq_b, reads=self.CQ_b, writes=[cq_b])
            cx.dma(ckT[:], self.CKV.rearrange("(c p) t -> p c t", p=128), ck_b, reads=self.CKV_b, writes=[ck_b])
            cx.dma(krT[:], self.KR, kr_b, reads=self.KR_b, writes=[kr_b])
            cx.dma(cs[:], self.CS.rearrange("w p t -> p w t"), cs_b, reads=[self.CS_b], writes=[cs_b])
            knT = cx.sb("knT", [128, S], BF); kn_b = cx.buf("knT")
            qnT = cx.sb("qnT", [128, S], BF); qn_b = cx.buf("qnT")
            qrT = cx.sb("qrT", [64, S], BF); qr_b = cx.buf("qrT")
            vtm = cx.sb("vtm", [128, S // 128, 128], BF); vt_b = cx.buf("vtm")
            wrot = cx.sb("wrotq", [128, 4, 64], BF); wrot_b = cx.buf("wrotq")
            qp = cx.sb("qp", [64, 2, TT], F32); qp_b = cx.buf("qp")
            PT = [cx.sb(f"PT{i}", [128, TT], BF) for i in range(2)]; PT_b = [cx.buf(f"PT{i}") for i in range(2)]
            rec = cx.sb("rec", [128, TT], F32); rec_b = cx.buf("rec")
            obt = [cx.sb(f"obt{i}", [128, TT], BF) for i in range(2)]; obt_b = [cx.buf(f"obt{i}") for i in range(2)]
            Wq, Wkv = I["mla_w_q_up"], I["mla_w_kv_up"]
            it = 0
            for h in range(MLA_H):
                wq, wqb = ph.load_w(Wq[:, h * 192:(h + 1) * 192].rearrange("(k p) n -> p k n", p=128), 4, 192)
                cx.op("act", lambda e: e.mul(out=wrot[:, :, 0:32], in_=wq[:, :, 160:192], mul=-1.0), reads=[wqb],
                      writes=[wrot_b])
                cx.op("act", lambda e: e.copy(out=wrot[:, :, 32:64], in_=wq[:, :, 128:160]), reads=[wqb], writes=[wrot_b],
                      accum=True)
                wk, wkb = ph.load_w(Wkv[:, h * 256:(h + 1) * 256].rearrange("(k p) n -> p k n", p=128), 4, 256)
                for tq in range(NT):
                    tsl = slice(tq * TT, (tq + 1) * TT)
                    for (wt, wtb, c0, src, srcb, dstT, dstb) in ((wk, wkb, 0, ckT, ck_b, knT, kn_b),
                                                                 (wq, wqb, 0, cqT, cq_b, qnT, qn_b)):
                        pb = ph.bank()
                        for k in range(4):
                            cx.op("pe", lambda e: e.matmul(self.ps[pb][:], lhsT=wt[:, k, c0:c0 + 128], rhs=src[:, k, tsl],
                                                           start=(k == 0), stop=(k == 3)),
                                  reads=[wtb, srcb], writes=[self.psb[pb]], inc=(k == 3), accum=(k > 0))
                        cx.op("act", lambda e: e.copy(out=dstT[:, tsl], in_=self.ps[pb][:]), reads=[self.psb[pb]],
                              writes=[dstb], accum=(tq > 0))
                    for wi, (wt, wtb, c0) in enumerate(((wq, wqb, 128), (wrot, wrot_b, 0))):
                        pb = ph.bank()
                        for k in range(4):
                            cx.op("pe", lambda e: e.matmul(self.ps[pb][0:64, :], lhsT=wt[:, k, c0:c0 + 64], rhs=cqT[:, k, tsl],
                                                           start=(k == 0), stop=(k == 3)),
                                  reads=[wtb, cq_b], writes=[self.psb[pb]], inc=(k == 3), accum=(k > 0))
                        cx.op("dve", lambda e: e.tensor_tensor(out=qp[:, wi, :], in0=self.ps[pb][0:64, :], in1=cs[:, wi, tsl],
                                                               op=ALU.mult),
                              reads=[self.psb[pb], cs_b], writes=[qp_b], accum=(wi > 0))
                    cx.op("dve", lambda e: e.tensor_tensor(out=qrT[:, tsl], in0=qp[:, 0, :], in1=qp[:, 1, :], op=ALU.add),
                          reads=[qp_b], writes=[qr_b], accum=(tq > 0))
                    pb = ph.bank()
                    for kk in range(4):
                        kt = tq * 4 + kk
                        for k in range(4):
                            cx.op("pe", lambda e: e.matmul(self.ps[pb][:, kk * 128:(kk + 1) * 128],
                                                           lhsT=ckT[:, k, kt * 128:(kt + 1) * 128], rhs=wk[:, k, 128:256],
                                                           start=(k == 0), stop=(k == 3), skip_group_check=True),
                                  reads=[wkb, ck_b], writes=[self.psb[pb]], inc=(k == 3 and kk == 3),
                                  accum=not (k == 0 and kk == 0))
                    cx.op("act", lambda e: e.copy(out=vtm[:, tq * 4:(tq + 1) * 4, :],
                                                  in_=self.ps[pb][:].rearrange("p (a b) -> p a b", b=128)),
                          reads=[self.psb[pb]], writes=[vt_b], accum=(tq > 0))
                for Q in range(NT):
                    po, pd = ph.bank(), ph.bank()
                    nkt = 4 * (Q + 1)
                    for kt in range(nkt):
                        d = kt - 4 * Q
                        c0 = max(d, 0) * 128
                        q0 = Q * TT + c0
                        ncol = TT - c0
                        pS = ph.bank()
                        while pS in (po, pd):
                            pS = ph.bank()
                        cx.op("pe", lambda e: e.matmul(self.ps[pS][:, 0:ncol], lhsT=knT[:, kt * 128:(kt + 1) * 128],
                                                       rhs=qnT[:, q0:q0 + ncol], start=True, stop=False),
                              reads=[kn_b, qn_b], writes=[self.psb[pS]], inc=False)
                        cx.op("pe", lambda e: e.matmul(self.ps[pS][:, 0:ncol], lhsT=krT[:, kt * 128:(kt + 1) * 128],
                                                       rhs=qrT[:, q0:q0 + ncol], start=False, stop=True),
                              reads=[kr_b, qr_b], writes=[self.psb[pS]], accum=True)
                        pt, ptb = PT[it % 2], PT_b[it % 2]
                        it += 1
                        cx.op("act", lambda e: e.activation(out=pt[:, 0:ncol], in_=self.ps[pS][:, 0:ncol], func=AF.Exp,
                                                            scale=SC), reads=[self.psb[pS]], writes=[ptb])
                        if d >= 0:
                            cx.op("dve", lambda e: e.tensor_tensor(out=pt[:, 0:128], in0=pt[:, 0:128], in1=self.cmask[:],
                                                                   op=ALU.mult), reads=[ptb, self.cmask_b], writes=[ptb])
                        first, last = (kt == 0), (kt == nkt - 1)
                        cx.op("pe", lambda e: e.matmul(self.ps[po][:, c0:TT], lhsT=vtm[:, kt, :], rhs=pt[:, 0:ncol],
                                                       start=first, stop=last),
                              reads=[vt_b, ptb], writes=[self.psb[po]], inc=last, accum=not first)
                        cx.op("pe", lambda e: e.matmul(self.ps[pd][:, c0:TT], lhsT=self.ones[:], rhs=pt[:, 0:ncol],
                                                       start=first, stop=last),
                              reads=[self.ones_b, ptb], writes=[self.psb[pd]], inc=True, accum=not first)
                    self.recip_bc(rec, rec_b, pd)
                    ot, otb = obt[Q % 2], obt_b[Q % 2]
                    cx.op("dve", lambda e: e.tensor_tensor(out=ot[:], in0=self.ps[po][:], in1=rec[:], op=ALU.mult),
                          reads=[self.psb[po], rec_b], writes=[otb])
                    cx.dma(self.OB[h * 128:(h + 1) * 128, Q * TT:(Q + 1) * TT], ot[:], otb, reads=[otb],
                           writes=[self.OB_b[Q]], partial=True)
            cx.end_phase(mark)
            cx.st = old_st

    def merge_phase(self, xres, xres_bufs, xout, xout_bufs):
        cx, I = self.cx, self.I
        with contextlib.ExitStack() as st:
            old_st = cx.st; mark = len(cx.dma_sems)
            ph = PH(self, st, None, I["mix_post_g"])
            obT = cx.sb("obT", [128, KC, TT], BF); obT_b = cx.buf("obT")
            yT = cx.sb("yT", [128, KC, TT], BF); yT_b = [cx.buf(f"yT{i}") for i in range(KC)]
            gy = [cx.sb(f"gy{i}", [128, 2, TT], BF) for i in range(2)]; gy_b = [cx.buf(f"gy{i}") for i in range(2)]
            tmp = cx.sb("mtmp", [128, TT], F32); tmp_b = cx.buf("mtmp")
            for T in range(NT):
                t0 = T * TT
                cx.dma(obT[:], self.OB[:, t0:t0 + TT].rearrange("(c p) t -> p c t", p=128), obT_b, reads=[self.OB_b[T]],
                       writes=[obT_b])

                def yb_cb(ci, pb):
                    g_, g_b = gy[ci % 2], gy_b[ci % 2]
                    cx.dma(g_[:, 0, :], self.GB[ci * 128:(ci + 1) * 128, t0:t0 + TT], g_b, reads=[self.GB_b[T]], writes=[g_b])
                    cx.dma(g_[:, 1, :], self.YA[ci * 128:(ci + 1) * 128, t0:t0 + TT], g_b, reads=[self.YA_b[T]], writes=[g_b],
                           partial=True)
                    cx.op("dve", lambda e: e.tensor_tensor(out=tmp[:], in0=self.ps[pb][:], in1=g_[:, 0, :], op=ALU.mult),
                          reads=[self.psb[pb], g_b], writes=[tmp_b])
                    cx.op("dve", lambda e: e.tensor_tensor(out=yT[:, ci, :], in0=tmp[:], in1=g_[:, 1, :], op=ALU.add),
                          reads=[tmp_b, g_b], writes=[yT_b[ci]])
                ph.proj_fm(I["w_branch_b"], 0, D, lambda k: obT[:, k, :], [obT_b], yb_cb)
                ph.tm_proj_post(lambda j, s: yT[:, j, s * 128:(s + 1) * 128], lambda j: yT_b[j], KC, I["w_out"], T,
                                xres, xres_bufs, xout, xout_bufs, 1.0)
            cx.end_phase(mark)
            cx.st = old_st

    def xa_phase(self, xin, xin_bufs, xout, xout_bufs):
        cx, I = self.cx, self.I
        SC = 128 ** -0.5
        with contextlib.ExitStack() as st:
            old_st = cx.st; mark = len(cx.dma_sems)
            ph = PH(self, st, I["xa_pre_g"], I["xa_post_g"])
            gm = ph.xy[:, 0, :]
            gm_b = ph.xy_b[0]
            cx.dma(gm, I["xa_mem_g"].partition_broadcast(128), gm_b, writes=[gm_b])
            mT = cx.sb("mT", [128, KC, N_MEM], BF); mT_b = cx.buf("mT")
            kmT = cx.sb("kmT", [128, XA_H, N_MEM], BF); km_b = cx.buf("kmT")
            vm = cx.sb("vm", [128, 2, 512], BF); vm_b = cx.buf("vm")
            qT = cx.sb("qT", [128, XA_H, TT], BF); qT_b = [cx.buf(f"qT{i}") for i in range(XA_H)]
            oT = cx.sb("oT", [128, XA_H, TT], BF); oT_b = [cx.buf(f"oT{i}") for i in range(XA_H)]
            PT = [cx.sb(f"PTx{i}", [128, TT], BF) for i in range(2)]; PT_b = [cx.buf(f"PTx{i}") for i in range(2)]
            rec = cx.sb("recx", [128, TT], F32); rec_b = cx.buf("recx")
            for s in range(N_MEM // 128):
                xt, xb = ph.xr[s % 2], ph.xr_b[s % 2]
                cx.dma(xt[:], I["mem"][s * 128:(s + 1) * 128, :], xb, writes=[xb])
                hb, hbb = ph.h0[s % 2], ph.h0_b[s % 2]
                junk, junk_b = ph.nextjunk()
                cx.op("act", lambda e: e.activation(out=junk[:], in_=xt[:], func=AF.Square, accum_out=ph.ssp[:, s:s + 1]),
                      reads=[xb], writes=[junk_b, ph.ssp_b])
                self.rstd_from_ss(ph.ssp[:, s:s + 1], ph.rs[:, s:s + 1], D, [ph.ssp_b], [ph.rs_b])
                cx.op("dve", lambda e: e.scalar_tensor_tensor(out=hb[:], in0=xt[:], scalar=ph.rs[:, s:s + 1], in1=gm,
                                                              op0=ALU.mult, op1=ALU.mult),
                      reads=[xb, ph.rs_b, gm_b], writes=[hbb])
                self.transpose_to(hb, hbb, KC, lambda c0, n: mT[:, c0:c0 + n, s * 128:(s + 1) * 128], mT_b)
                mT_b.w = list(mT_b.w)

            def km_cb(ci, pb):
                cx.op("act", lambda e: e.copy(out=kmT[:, ci, :], in_=self.ps[pb][:, 0:N_MEM]), reads=[self.psb[pb]],
                      writes=[km_b], accum=(ci > 0))
            ph.proj_fm(I["xa_w_k"], 0, 512, lambda k: mT[:, k, :], [mT_b], km_cb, nfree=N_MEM)
            wv_, wvb = ph.load_wcols(I["xa_w_v"], 0, 256)
            wv2, wvb2 = ph.load_wcols(I["xa_w_v"], 256, 256)
            for kt in range(2):
                pb = ph.bank()
                for hf, (wt, wtb) in enumerate(((wv_, wvb), (wv2, wvb2))):
                    for k in range(KC):
                        cx.op("pe", lambda e: e.matmul(self.ps[pb][:, hf * 256:(hf + 1) * 256],
                                                       lhsT=mT[:, k, kt * 128:(kt + 1) * 128], rhs=wt[:, k, :],
                                                       start=(k == 0), stop=(k == KC - 1), skip_group_check=True),
                              reads=[wtb, mT_b], writes=[self.psb[pb]], inc=(k == KC - 1 and hf == 1),
                              accum=not (k == 0 and hf == 0))
                cx.op("act", lambda e: e.copy(out=vm[:, kt, :], in_=self.ps[pb][:]), reads=[self.psb[pb]], writes=[vm_b],
                      accum=(kt > 0))
            it = 0
            for T in range(NT):
                ph.prenorm(xin, xin_bufs, T)

                def q_cb(ci, pb):
                    cx.op("act", lambda e: e.copy(out=qT[:, ci, :], in_=self.ps[pb][:]), reads=[self.psb[pb]],
                          writes=[qT_b[ci]])
                ph.proj_fm(I["xa_w_q"], 0, 512, lambda k: ph.hT[:, k, :], ph.hT_b, q_cb)
                for hh in range(XA_H):
                    po, pd = ph.bank(), ph.bank()
                    for kt in range(2):
                        pS = ph.bank()
                        cx.op("pe", lambda e: e.matmul(self.ps[pS][:], lhsT=kmT[:, hh, kt * 128:(kt + 1) * 128],
                                                       rhs=qT[:, hh, :], start=True, stop=True),
                              reads=[km_b, qT_b[hh]], writes=[self.psb[pS]])
                        pt, ptb = PT[it % 2], PT_b[it % 2]
                        it += 1
                        cx.op("act", lambda e: e.activation(out=pt[:], in_=self.ps[pS][:], func=AF.Exp, scale=SC),
                              reads=[self.psb[pS]], writes=[ptb])
                        cx.op("pe", lambda e: e.matmul(self.ps[po][:], lhsT=vm[:, kt, hh * 128:(hh + 1) * 128], rhs=pt[:],
                                                       start=(kt == 0), stop=(kt == 1)),
                              reads=[vm_b, ptb], writes=[self.psb[po]], accum=(kt > 0))
                        cx.op("pe", lambda e: e.matmul(self.ps[pd][:], lhsT=self.ones[:], rhs=pt[:],
                                                       start=(kt == 0), stop=(kt == 1)),
                              reads=[self.ones_b, ptb], writes=[self.psb[pd]], accum=(kt > 0))
                    self.recip_bc(rec, rec_b, pd)
                    cx.op("dve", lambda e: e.tensor_tensor(out=oT[:, hh, :], in0=self.ps[po][:], in1=rec[:], op=ALU.mult),
                          reads=[self.psb[po], rec_b], writes=[oT_b[hh]])
                ph.tm_proj_post(lambda j, s: oT[:, j, s * 128:(s + 1) * 128], lambda j: oT_b[j], XA_H, I["xa_w_o"], T,
                                xin, xin_bufs, xout, xout_bufs, 1.0)
            cx.end_phase(mark)
            cx.st = old_st


_CACHE = {}


def _consts():
    idx = np.arange(128)
    tri = ((idx[:, None] <= idx[None, :]) & ((idx[:, None] // 64) == (idx[None, :] // 64))).astype(np.float32)
    cm = np.ones((128, 128), np.float32)
    cm[64:, :64] = 0.0
    invf = (1.0 / (np.float32(10000.0) ** (np.arange(0, 64, 2, dtype=np.float32) / np.float32(64)))).astype(np.float32)
    invf = np.concatenate([invf, invf]).reshape(64, 1).astype(np.float32)
    return {"ident_in": np.eye(128, dtype=np.float32), "tri_in": np.tile(tri, (1, 4)), "cmask_in": cm, "invf_in": invf}


def kernel(**inputs):
    inputs = {k: np.asarray(v) for k, v in inputs.items()}
    if "nc" not in _CACHE:
        _CACHE["nc"] = Prog().build()
    nc = _CACHE["nc"]
    shared = dict(_consts())
    for k, v in inputs.items():
        if k in ("x", "mem", "positions"):
            continue
        if k == "hgrn_lb_logits":
            shared[k] = np.ascontiguousarray(v)
        elif v.ndim == 2:
            shared[k] = np.ascontiguousarray(v[0:1])
        else:
            shared[k] = np.ascontiguousarray(v[0])
    in_maps = []
    for c in range(8):
        m = dict(shared)
        m["x"] = np.ascontiguousarray(inputs["x"][c])
        m["mem"] = np.ascontiguousarray(inputs["mem"][c])
        m["positions"] = np.ascontiguousarray(inputs["positions"][c:c + 1]).astype(np.int32)
        in_maps.append(m)
    res = run_bass_kernel_spmd(nc, in_maps, core_ids=list(range(8)))
    out = np.stack([np.asarray(r["out"]) for r in res.results], axis=0)
    return out.astype(np.float32)
```
